# Optimizing a Trainium2 kernel written in Bass

```python
import math
import jax, jax.numpy as jnp
from jax import lax
import numpy as np

D_MODEL = 1024
BATCH = 8
SEQ = 2048
DEPTH = 4

PLE_DIM = 256
GRID_W = 64
Q_BLOCK = 128
ROPE_THETA = 10000.0
EPS = 1e-6
CONV_CH = D_MODEL // 4
CONV_WIDTH = 31
DIFF_HEADS = 4
DIFF_QK_DIM = 32
DIFF_V_DIM = 2 * DIFF_QK_DIM
GQA_HEADS = 8
GQA_KV_HEADS = 2
GQA_HEAD_DIM = 64
FFN_DIM = 2816
FFN_CONV_WIDTH = 3
N_BRANCH = 3

SEG_SIZES = (
    2 * CONV_CH,
    DIFF_HEADS * 2 * DIFF_QK_DIM,
    DIFF_HEADS * 2 * DIFF_QK_DIM,
    DIFF_HEADS * DIFF_V_DIM,
    GQA_HEADS * GQA_HEAD_DIM,
    GQA_KV_HEADS * GQA_HEAD_DIM,
    GQA_KV_HEADS * GQA_HEAD_DIM,
    N_BRANCH * D_MODEL,
)
IN_COLS = sum(SEG_SIZES)
SPLIT_POINTS = tuple(int(v) for v in np.cumsum(SEG_SIZES)[:-1])

kernel_name = "hybrid_gated_conv_diffattn_axialgqa_encoder"


def _rms(x, g):
    x32 = x.astype(jnp.float32)
    y = x32 * lax.rsqrt(jnp.mean(x32 * x32, axis=-1, keepdims=True) + EPS)
    return (y * g.astype(jnp.float32)).astype(x.dtype)


def _layernorm(x, g, b):
    x32 = x.astype(jnp.float32)
    mu = jnp.mean(x32, axis=-1, keepdims=True)
    var = jnp.mean(jnp.square(x32 - mu), axis=-1, keepdims=True)
    y = (x32 - mu) * lax.rsqrt(var + EPS)
    return (y * g.astype(jnp.float32) + b.astype(jnp.float32)).astype(x.dtype)


def _inv_freq(dim):
    return 1.0 / (ROPE_THETA ** (jnp.arange(0, dim, 2, dtype=jnp.float32) / dim))


def _rope(x, ang):
    d = x.shape[-1]
    cos = jnp.cos(ang)[None, :, None, :].astype(x.dtype)
    sin = jnp.sin(ang)[None, :, None, :].astype(x.dtype)
    xp = x.reshape(x.shape[:-1] + (d // 2, 2))
    x0, x1 = xp[..., 0], xp[..., 1]
    out = jnp.stack([x0 * cos - x1 * sin, x0 * sin + x1 * cos], axis=-1)
    return out.reshape(x.shape)


def _dwconv(u, w, b):
    width, ch = w.shape
    pad = width // 2
    y = lax.conv_general_dilated(
        u, w[:, None, :].astype(u.dtype), window_strides=(1,), padding=[(pad, pad)],
        dimension_numbers=("NWC", "WIO", "NWC"), feature_group_count=ch)
    return y + b.astype(u.dtype)


def _to_blocks(t):
    b, s = t.shape[:2]
    t = t.reshape((b, s // Q_BLOCK, Q_BLOCK) + t.shape[2:])
    return jnp.moveaxis(t, 1, 0)


def _from_blocks(t):
    t = jnp.moveaxis(t, 0, 1)
    return t.reshape((t.shape[0], t.shape[1] * t.shape[2]) + t.shape[3:])


def _diff_attention(q1, q2, k1, k2, v, lam):
    scale = DIFF_QK_DIM ** -0.5

    def blk(qs):
        qb1, qb2 = qs
        s1 = jnp.einsum("bqhd,bkhd->bhqk", qb1, k1).astype(jnp.float32) * scale
        s2 = jnp.einsum("bqhd,bkhd->bhqk", qb2, k2).astype(jnp.float32) * scale
        a = jax.nn.softmax(s1, axis=-1) - lam * jax.nn.softmax(s2, axis=-1)
        return jnp.einsum("bhqk,bkhd->bqhd", a.astype(v.dtype), v)

    out = lax.map(blk, (_to_blocks(q1), _to_blocks(q2)))
    return _from_blocks(out)


def _gqa_attention(q, k, v):
    b, s, hq, d = q.shape
    g = GQA_KV_HEADS
    r = hq // g
    scale = d ** -0.5
    qg = q.reshape(b, s, g, r, d)

    def blk(qb):
        sc = jnp.einsum("bqgrd,bkgd->bgrqk", qb, k).astype(jnp.float32) * scale
        pr = jax.nn.softmax(sc, axis=-1).astype(v.dtype)
        return jnp.einsum("bgrqk,bkgd->bqgrd", pr, v)

    out = _from_blocks(lax.map(blk, _to_blocks(qg)))
    return out.reshape(b, s, hq, d)


def setup_inputs(seed: int = 0) -> dict:
    key = jax.random.key(seed)
    ks = jax.random.split(key, 32)
    f32 = jnp.float32

    def w(k, shape, fan_in):
        return jax.random.normal(k, shape, f32) * (fan_in ** -0.5)

    def gain(k, shape):
        return 1.0 + 0.05 * jax.random.normal(k, shape, f32)

    def bias(k, shape):
        return 0.01 * jax.random.normal(k, shape, f32)

    L = DEPTH
    return {
        "x": jax.random.normal(ks[0], (BATCH, SEQ, D_MODEL), f32),
        "p": jax.random.normal(ks[1], (DEPTH, BATCH, SEQ, PLE_DIM), f32),
        "norm_mix_pre": gain(ks[2], (L, D_MODEL)),
        "norm_mix_post": gain(ks[3], (L, D_MODEL)),
        "w_in": w(ks[4], (L, D_MODEL, IN_COLS), D_MODEL),
        "conv_dw_w": w(ks[5], (L, CONV_WIDTH, CONV_CH), CONV_WIDTH),
        "conv_dw_b": bias(ks[6], (L, CONV_CH)),
        "conv_ln_g": gain(ks[7], (L, CONV_CH)),
        "conv_ln_b": bias(ks[8], (L, CONV_CH)),
        "w_conv_out": w(ks[9], (L, CONV_CH, D_MODEL), CONV_CH),
        "diff_lambda": 0.1 * jax.random.normal(ks[10], (L, 4, DIFF_QK_DIM), f32),
        "diff_subln_g": gain(ks[11], (L, DIFF_V_DIM)),
        "w_diff_out": w(ks[12], (L, DIFF_HEADS * DIFF_V_DIM, D_MODEL), DIFF_HEADS * DIFF_V_DIM),
        "gqa_q_norm": gain(ks[13], (L, GQA_HEAD_DIM)),
        "gqa_k_norm": gain(ks[14], (L, GQA_HEAD_DIM)),
        "w_gqa_out": w(ks[15], (L, GQA_HEADS * GQA_HEAD_DIM, D_MODEL), GQA_HEADS * GQA_HEAD_DIM),
        "w_out": w(ks[16], (L, D_MODEL, D_MODEL), D_MODEL),
        "norm_ffn_pre": gain(ks[17], (L, D_MODEL)),
        "norm_ffn_post": gain(ks[18], (L, D_MODEL)),
        "w_up": w(ks[19], (L, D_MODEL, 2 * FFN_DIM), D_MODEL),
        "ffn_dw_w": w(ks[20], (L, FFN_CONV_WIDTH, 2 * FFN_DIM), FFN_CONV_WIDTH),
        "ffn_dw_b": bias(ks[21], (L, 2 * FFN_DIM)),
        "w_down": w(ks[22], (L, FFN_DIM, D_MODEL), FFN_DIM),
        "w_ple": w(ks[23], (L, PLE_DIM, D_MODEL), PLE_DIM),
        "w_ple_gate": w(ks[24], (L, D_MODEL, D_MODEL), D_MODEL),
    }


def reference(x, p, norm_mix_pre, norm_mix_post, w_in, conv_dw_w, conv_dw_b, conv_ln_g,
              conv_ln_b, w_conv_out, diff_lambda, diff_subln_g, w_diff_out, gqa_q_norm,
              gqa_k_norm, w_gqa_out, w_out, norm_ffn_pre, norm_ffn_post, w_up, ffn_dw_w,
              ffn_dw_b, w_down, w_ple, w_ple_gate):
    B, S, D = x.shape
    ROWS = S // GRID_W
    t = jnp.arange(S, dtype=jnp.float32)
    ang_1d = t[:, None] * _inv_freq(DIFF_QK_DIM)[None, :]
    half = GQA_HEAD_DIM // 2
    row = jnp.repeat(jnp.arange(ROWS, dtype=jnp.float32), GRID_W)
    col = jnp.tile(jnp.arange(GRID_W, dtype=jnp.float32), ROWS)
    fr = _inv_freq(half)
    ang_2d = jnp.concatenate([row[:, None] * fr[None, :], col[:, None] * fr[None, :]], axis=-1)

    for i in range(DEPTH):
        h = _rms(x, norm_mix_pre[i])
        proj = h @ w_in[i]
        (a_in, dq, dk, dv, gq, gk, gv, gates) = jnp.split(proj, SPLIT_POINTS, axis=-1)

        a_val, a_gate = jnp.split(a_in, 2, axis=-1)
        u = a_val * jax.nn.sigmoid(a_gate)
        u = _dwconv(u, conv_dw_w[i], conv_dw_b[i])
        u = jax.nn.silu(_layernorm(u, conv_ln_g[i], conv_ln_b[i]))
        br_a = u @ w_conv_out[i]

        dq = dq.reshape(B, S, DIFF_HEADS, 2, DIFF_QK_DIM)
        dk = dk.reshape(B, S, DIFF_HEADS, 2, DIFF_QK_DIM)
        dv = dv.reshape(B, S, DIFF_HEADS, DIFF_V_DIM)
        q1, q2 = _rope(dq[..., 0, :], ang_1d), _rope(dq[..., 1, :], ang_1d)
        k1, k2 = _rope(dk[..., 0, :], ang_1d), _rope(dk[..., 1, :], ang_1d)
        lam_init = 0.8 - 0.6 * math.exp(-0.3 * i)
        lp = diff_lambda[i].astype(jnp.float32)
        lam = jnp.exp(jnp.sum(lp[0] * lp[1])) - jnp.exp(jnp.sum(lp[2] * lp[3])) + lam_init
        o_b = _diff_attention(q1, q2, k1, k2, dv, lam)
        o_b = _rms(o_b, diff_subln_g[i]) * (1.0 - lam_init)
        br_b = o_b.reshape(B, S, DIFF_HEADS * DIFF_V_DIM) @ w_diff_out[i]

        gq = _rms(gq.reshape(B, S, GQA_HEADS, GQA_HEAD_DIM), gqa_q_norm[i])
        gk = _rms(gk.reshape(B, S, GQA_KV_HEADS, GQA_HEAD_DIM), gqa_k_norm[i])
        gv = gv.reshape(B, S, GQA_KV_HEADS, GQA_HEAD_DIM)
        o_c = _gqa_attention(_rope(gq, ang_2d), _rope(gk, ang_2d), gv)
        br_c = o_c.reshape(B, S, GQA_HEADS * GQA_HEAD_DIM) @ w_gqa_out[i]

        g = jax.nn.sigmoid(gates).reshape(B, S, N_BRANCH, D)
        merged = g[:, :, 0] * br_a + g[:, :, 1] * br_b + g[:, :, 2] * br_c
        x = x + _rms(merged @ w_out[i], norm_mix_post[i])

        h2 = _rms(x, norm_ffn_pre[i])
        up = _dwconv(h2 @ w_up[i], ffn_dw_w[i], ffn_dw_b[i])
        f_gate, f_val = jnp.split(up, 2, axis=-1)
        ffn = (jax.nn.gelu(f_gate, approximate=True) * f_val) @ w_down[i]
        x = x + _rms(ffn, norm_ffn_post[i])

        x = x + jax.nn.sigmoid(x @ w_ple_gate[i]) * (p[i] @ w_ple[i])
    return x
```

```python
from contextlib import ExitStack
import math
import numpy as np
import concourse.bass as bass
import concourse.mybir as mybir
from concourse.bass_utils import run_bass_kernel_spmd

F32 = mybir.dt.float32
BF16 = mybir.dt.bfloat16
AF = mybir.ActivationFunctionType
ALU = mybir.AluOpType
AX = mybir.AxisListType

D = 1024
S = 2048
L = 4
NCORES = 8
PLE = 256
CONV_CH = 256
CONV_W = 31
FFN = 2816
EPS = 1e-6
TT = 4
NKC = 8
SLOT = 2048
NSLOT = 4
PREFETCH = 2

ENGS = ("pe", "act", "dve", "pool", "sp")


class _Rec:
    def __getattr__(self, name):
        def f(*a, **k):
            return lambda e: getattr(e, name)(*a, **k)
        return f


E = _Rec()
SAME_ENG_DIST = 3


class Op:
    __slots__ = ("eng", "fn", "reads", "writes", "dma", "slot", "idx", "eidx",
                 "waits", "signal", "sigval", "sem")

    def __init__(self, eng, fn, reads, writes, dma=False, slot=None):
        self.eng = eng
        self.fn = fn
        self.reads = tuple(reads)
        self.writes = tuple(writes)
        self.dma = dma
        self.slot = slot
        self.waits = {}
        self.signal = False
        self.sigval = 0
        self.sem = None


class Prog:
    def __init__(self, nc):
        self.nc = nc
        self.ops = []
        self.stack = ExitStack()
        self.finals = []
        self.nbar = 0
        self.bar_extra = {}

    def sb(self, name, shape, dtype):
        return self.stack.enter_context(self.nc.sbuf_tensor(name, list(shape), dtype))

    def ps(self, name, shape, dtype=F32):
        return self.stack.enter_context(self.nc.psum_tensor(name, list(shape), dtype))

    def op(self, eng, fn, reads=(), writes=()):
        psr = [r for r in reads if isinstance(r, tuple) and r and r[0] == "ps"]
        if psr:
            reads = [r for r in reads if not (isinstance(r, tuple) and r and r[0] == "ps")]
            writes = list(writes) + psr
        o = Op(eng, fn, reads, writes)
        self.ops.append(o)
        return o

    def dma(self, eng, fn, reads=(), writes=(), slot=None, final=False):
        if slot is None:
            slot = writes[0]
        o = Op(eng, fn, reads, writes, dma=True, slot=("dma", slot))
        o.signal = True
        if final:
            self.finals.append(("dma", slot))
        self.ops.append(o)
        return o

    def barrier(self, markers):
        n = self.nbar
        self.nbar += 1
        for e, (fn, ex) in markers.items():
            self.op(e, fn, writes=[("bar", n, e)] + list(ex))
        for e, (fn, ex) in markers.items():
            if e == "pe":
                continue
            self.op(e, fn, reads=[("bar", n, e2) for e2 in markers if e2 != e],
                    writes=[("bar2", n, e)] + list(ex))

    def finish(self):
        nc = self.nc
        ops = self.ops
        last_w = {}
        readers = {}
        ecount = {e: 0 for e in ENGS}
        for i, o in enumerate(ops):
            o.idx = i
            o.eidx = ecount[o.eng]
            ecount[o.eng] += 1
        need = [[] for _ in ops]
        for o in ops:
            ds = set()
            for r in o.reads:
                w = last_w.get(r)
                if w is not None:
                    ds.add(w)
            for r in o.writes:
                w = last_w.get(r)
                if w is not None:
                    ds.add(w)
                for rd in readers.get(r, ()):
                    ds.add(rd)
            for r in o.reads:
                readers.setdefault(r, []).append(o.idx)
            for r in o.writes:
                last_w[r] = o.idx
                readers[r] = []
            ds.discard(o.idx)
            for d in sorted(ds):
                a = ops[d]
                if a.dma:
                    need[o.idx].append(d)
                elif a.eng == o.eng:
                    if a.eng in ("pe", "sp"):
                        continue
                    if o.eidx - a.eidx < SAME_ENG_DIST:
                        need[o.idx].append(d)
                else:
                    need[o.idx].append(d)
        for o in ops:
            for d in need[o.idx]:
                ops[d].signal = True
        semcount = {}
        for o in ops:
            if not o.signal:
                continue
            key = o.slot if o.dma else ("eng", o.eng)
            inc = 16 if o.dma else 1
            semcount[key] = semcount.get(key, 0) + inc
            o.sem = key
            o.sigval = semcount[key]
        waited = {e: {} for e in ENGS}
        for o in ops:
            w = {}
            for d in need[o.idx]:
                a = ops[d]
                if waited[o.eng].get(a.sem, 0) >= a.sigval:
                    continue
                if w.get(a.sem, 0) < a.sigval:
                    w[a.sem] = a.sigval
            for k, v in w.items():
                waited[o.eng][k] = v
            o.waits = w
        semkeys = sorted(semcount.keys(), key=str)
        sems = {}
        for i, k in enumerate(semkeys):
            sems[k] = self.stack.enter_context(nc.semaphore("sem%d" % i))
        engobj = {"pe": "tensor", "act": "scalar", "dve": "vector", "pool": "gpsimd", "sp": "sync"}
        by_eng = {e: [o for o in ops if o.eng == e] for e in ENGS}
        final = set(self.finals)
        self.n_sems = len(semkeys)

        def make(ename):
            def body(eng):
                for o in by_eng[ename]:
                    for k, v in o.waits.items():
                        eng.wait_ge(sems[k], v)
                    ins = o.fn(eng)
                    if o.signal:
                        ins.then_inc(sems[o.sem], 16 if o.dma else 1)
                if ename == "sp":
                    for k in semkeys:
                        if k in final:
                            eng.wait_ge(sems[k], semcount[k])
            return body

        with nc.Block() as block:
            for ename in ENGS:
                if not by_eng[ename] and ename != "sp":
                    continue
                getattr(block, engobj[ename])(make(ename))
        self.stack.close()
        return nc


def _perm_half(d):
    return np.concatenate([np.arange(0, d, 2), np.arange(1, d, 2)])


O_AVAL, O_AGATE = 0, 256
O_DQ, O_DK, O_DV = 512, 768, 1024
O_GQ, O_GK, O_GV = 1280, 1792, 1920
O_GATES = 2048


def _win_units():
    u = {}
    u["ag0"] = O_AGATE + np.arange(128)
    u["ag1"] = O_AGATE + 128 + np.arange(128)
    u["av0"] = O_AVAL + np.arange(128)
    u["av1"] = O_AVAL + 128 + np.arange(128)
    p32 = _perm_half(32)
    p64 = _perm_half(64)
    for c in range(2):
        cols = -np.ones(128, np.int64)
        for hh in range(2):
            for s_ in range(2):
                cols[hh * 64 + s_ * 32: hh * 64 + s_ * 32 + 32] = O_DQ + ((2 * c + hh) * 2 + s_) * 32 + p32
        u["dq%d" % c] = cols
        for s_ in range(2):
            cols = -np.ones(128, np.int64)
            for hh in range(2):
                cols[hh * 64 + s_ * 32: hh * 64 + s_ * 32 + 32] = O_DK + ((2 * c + hh) * 2 + s_) * 32 + p32
            u["dk%d" % (2 * s_ + c)] = cols
    for c in range(4):
        cols = np.concatenate([O_GQ + (2 * c) * 64 + p64, O_GQ + (2 * c + 1) * 64 + p64])
        u["gq%d" % c] = cols
    for g in range(2):
        cols = np.concatenate([O_GK + g * 64 + p64, O_GK + g * 64 + p64])
        u["gk%d" % g] = cols
    return u


def _unit_fm(W, cols):
    K = W.shape[0]
    kc = K // 128
    Wc = np.zeros((K, len(cols)), np.float32)
    valid = cols >= 0
    Wc[:, valid] = W[:, cols[valid]]
    return np.ascontiguousarray(Wc.reshape(kc, 128, len(cols)).transpose(1, 0, 2)).reshape(128, -1)


def layer_tiles():
    t = []
    for nm in ("ag", "av", "dqa", "dqb", "dkb", "vd", "gqa", "gqb", "gk", "vg"):
        t.append((nm, 1024 if nm == "vg" else 2048))
    for j in range(8):
        t.append(("mgA%d" % j, 2048))
        t.append(("mgB%d" % j, 2048))
    for i in range(4):
        t.append(("wo%d" % i, 2048))
    for i in range(22):
        t.append(("up%d" % i, 2048))
    for j in range(8):
        t.append(("dn%da" % j, 1408))
        t.append(("dn%db" % j, 1408))
    t.append(("ple", 2048))
    for i in range(4):
        t.append(("pg%d" % i, 2048))
    return t


def tile_offsets():
    offs = {}
    o = 0
    for nm, w in layer_tiles():
        offs[nm] = (o, w)
        o += w
    return offs, o


def pack_weights(inp, nl=L):
    offs, LW = tile_offsets()
    out = np.zeros((128, nl * LW), np.float32)
    wu = _win_units()
    for l in range(nl):
        w_in = np.asarray(inp["w_in"][l])
        base = l * LW

        def put(nm, arr):
            o, w = offs[nm]
            assert arr.shape == (128, w), (nm, arr.shape, w)
            out[:, base + o: base + o + w] = arr

        def two(a, b):
            return np.concatenate([_unit_fm(w_in, wu[a]), _unit_fm(w_in, wu[b])], axis=1)

        put("ag", two("ag0", "ag1"))
        put("av", two("av0", "av1"))
        put("dqa", two("dq0", "dq1"))
        put("dqb", two("dk0", "dk1"))
        put("dkb", two("dk2", "dk3"))
        put("vd", _unit_fm(w_in, O_DV + np.arange(256)))
        put("gqa", two("gq0", "gq1"))
        put("gqb", two("gq2", "gq3"))
        put("gk", two("gk0", "gk1"))
        put("vg", _unit_fm(w_in, O_GV + np.arange(128)))
        wco = np.asarray(inp["w_conv_out"][l])
        wdo = np.asarray(inp["w_diff_out"][l])
        wgo = np.asarray(inp["w_gqa_out"][l])
        for j in range(8):
            cj = j * 128 + np.arange(128)
            gA = _unit_fm(w_in, O_GATES + cj)
            gB = _unit_fm(w_in, O_GATES + 1024 + cj)
            gC = _unit_fm(w_in, O_GATES + 2048 + cj)
            ow = np.concatenate([_unit_fm(wco, cj), _unit_fm(wdo, cj), _unit_fm(wgo, cj)], axis=1)
            put("mgA%d" % j, np.concatenate([gA, gB], axis=1))
            put("mgB%d" % j, np.concatenate([gC, ow], axis=1))
        w_out = np.asarray(inp["w_out"][l])
        for i in range(4):
            put("wo%d" % i, np.concatenate([_unit_fm(w_out, (2 * i) * 128 + np.arange(128)),
                                            _unit_fm(w_out, (2 * i + 1) * 128 + np.arange(128))], axis=1))
        w_up = np.asarray(inp["w_up"][l])
        for i in range(22):
            put("up%d" % i, np.concatenate([_unit_fm(w_up, i * 128 + np.arange(128)),
                                            _unit_fm(w_up, FFN + i * 128 + np.arange(128))], axis=1))
        w_dn = np.asarray(inp["w_down"][l])
        for j in range(8):
            full = _unit_fm(w_dn, j * 128 + np.arange(128)).reshape(128, 22, 128)
            put("dn%da" % j, full[:, :11].reshape(128, -1))
            put("dn%db" % j, full[:, 11:].reshape(128, -1))
        put("ple", _unit_fm(np.asarray(inp["w_ple"][l]), np.arange(1024)))
        wpg = np.asarray(inp["w_ple_gate"][l])
        for i in range(4):
            put("pg%d" % i, np.concatenate([_unit_fm(wpg, (2 * i) * 128 + np.arange(128)),
                                            _unit_fm(wpg, (2 * i + 1) * 128 + np.arange(128))], axis=1))
    return out, offs, LW


def prm_layout():
    o = {}
    n = 0

    def add(nm, w):
        nonlocal n
        o[nm] = (n, w)
        n += w
    add("nmp", 8)
    add("nmpost", 8)
    add("nfp", 8)
    add("nfpost", 8)
    add("cw", 62)
    add("cb", 2)
    add("clg", 2)
    add("clb", 2)
    add("subg", 1)
    add("gqn", 1)
    add("gkn", 1)
    add("fw", 132)
    add("fb", 44)
    add("dl", 128)
    return o, n


def pack_params(inp):
    lay, n = prm_layout()
    out = np.zeros((128, L * n), np.float32)
    p64 = _perm_half(64)
    for l in range(L):
        b = l * n

        def put(nm, arr):
            o, w = lay[nm]
            out[:, b + o: b + o + w] = np.asarray(arr, np.float32).reshape(128, w)

        def fm(v):
            v = np.asarray(v)
            return v.reshape(-1, 128).T
        put("nmp", fm(inp["norm_mix_pre"][l]))
        put("nmpost", fm(inp["norm_mix_post"][l]))
        put("nfp", fm(inp["norm_ffn_pre"][l]))
        put("nfpost", fm(inp["norm_ffn_post"][l]))
        cw = np.asarray(inp["conv_dw_w"][l])
        put("cw", np.concatenate([cw[:, 0:128].T, cw[:, 128:256].T], axis=1))
        put("cb", fm(inp["conv_dw_b"][l]))
        put("clg", fm(inp["conv_ln_g"][l]))
        put("clb", fm(inp["conv_ln_b"][l]))
        put("subg", np.tile(np.asarray(inp["diff_subln_g"][l]), 2).reshape(128, 1))
        put("gqn", np.tile(np.asarray(inp["gqa_q_norm"][l])[p64], 2).reshape(128, 1))
        put("gkn", np.tile(np.asarray(inp["gqa_k_norm"][l])[p64], 2).reshape(128, 1))
        fw = np.asarray(inp["ffn_dw_w"][l])
        put("fw", np.concatenate([fw[k].reshape(44, 128).T for k in range(3)], axis=1))
        put("fb", fm(inp["ffn_dw_b"][l]))
        put("dl", np.tile(np.asarray(inp["diff_lambda"][l]).reshape(1, 128), (128, 1)))
    return out, lay, n


def const_tables():
    ones = np.ones((128, 128), np.float32)
    ident = np.eye(128, dtype=np.float32)
    blk64 = np.zeros((128, 128), np.float32)
    blk64[0:64, 0:64] = 1
    blk64[64:128, 64:128] = 1
    r32 = np.zeros((128, 128), np.float32)
    r64 = np.zeros((128, 128), np.float32)
    for m in range(128):
        r32[(m // 32) * 32 + ((m % 32) + 16) % 32, m] = 1
        r64[(m // 64) * 64 + ((m % 64) + 32) % 64, m] = 1
    cm = np.concatenate([ones, ident, blk64, r32, r64], axis=1)
    t = np.arange(S, dtype=np.float32)
    f32_ = (1.0 / (10000.0 ** (np.arange(0, 32, 2, dtype=np.float32) / 32))).astype(np.float32)
    p = np.arange(128)
    ang1 = t[None, :] * f32_[p % 16][:, None]
    cos1 = np.cos(ang1)
    sin1 = np.sin(ang1) * np.where((p % 32) < 16, -1.0, 1.0)[:, None]
    row = np.floor(t / 64.0)
    col = t - 64.0 * row
    i = p % 32
    pos = np.where((i < 16)[:, None], row[None, :], col[None, :])
    ang2 = pos * f32_[i % 16][:, None]
    cos2 = np.cos(ang2)
    sin2 = np.sin(ang2) * np.where((p % 64) < 32, -1.0, 1.0)[:, None]
    rope = np.concatenate([cos1, sin1, cos2, sin2], axis=1).astype(np.float32)
    return cm.astype(np.float32), rope


def build(nl=L, dbg=None):
    nc = bass.Bass("TRN2", target_bir_lowering=False)
    P = Prog(nc)
    offs, LW = tile_offsets()
    play, NPRM = prm_layout()

    xT_d = nc.dram_tensor("xT", [8, 128, S], F32, kind="ExternalInput").ap()
    pT_d = nc.dram_tensor("pT", [L, 2, 128, S], F32, kind="ExternalInput").ap()
    ws_d = nc.dram_tensor("ws", [128, nl * LW], F32, kind="ExternalInput").ap()
    prm_d = nc.dram_tensor("prm", [128, L * NPRM], F32, kind="ExternalInput").ap()
    cm_d = nc.dram_tensor("cm", [128, 640], F32, kind="ExternalInput").ap()
    rope_d = nc.dram_tensor("rope", [128, 4 * S], F32, kind="ExternalInput").ap()
    yT_d = nc.dram_tensor("yT", [8, 128, S], F32, kind="ExternalOutput").ap()
    dbg_d = None
    if dbg is not None:
        dbg_d = nc.dram_tensor("dbg", [128, 8 * S], F32, kind="ExternalOutput").ap()

    xT = P.sb("xT_sb", [128, 8, S], F32)
    hT = P.sb("hT_sb", [128, 8, S], BF16)
    prm = P.sb("prm_sb", [128, L * NPRM], F32)
    cm = P.sb("cm_sb", [128, 640], BF16)
    lamt = P.sb("lam_sb", [128, L * 4], F32)
    ring = P.sb("ring_sb", [128, NSLOT, SLOT], BF16)
    BIGB = 84 * 1024
    big = P.sb("big_sb", [128, BIGB // 2], BF16)
    ps = P.ps("psum", [128, 8, 512], F32)

    ones_m = cm[:, 0:128]
    ident_m = cm[:, 128:256]
    blk64_m = cm[:, 256:384]
    r32_m = cm[:, 384:512]
    r64_m = cm[:, 512:640]

    def carve(off_b, shape, dtype):
        n = int(np.prod(shape))
        if dtype == BF16:
            assert off_b % 2 == 0
            a = big[:, off_b // 2: off_b // 2 + n]
        else:
            assert off_b % 4 == 0
            a = big[:, off_b // 2: off_b // 2 + 2 * n].bitcast(F32)
        if len(shape) == 1:
            return a
        if len(shape) == 2:
            return a.rearrange("p (a b) -> p a b", a=shape[0])
        if len(shape) == 3:
            return a.rearrange("p (a b c) -> p a b c", a=shape[0], b=shape[1])
        raise ValueError

    KB = 1024
    uA = carve(0, [2, S], BF16)
    oB = carve(8 * KB, [2, S], BF16)
    upad = carve(16 * KB, [2, S + 32], BF16)
    dg = carve(25 * KB, [2, CONV_W, 128], BF16)
    tab = carve(16 * KB, [2, S], BF16)
    qd = carve(24 * KB, [2, S], BF16)
    kd = carve(32 * KB, [4, S], BF16)
    vd = carve(48 * KB, [16, 384], BF16)
    gq = carve(24 * KB, [4, S], BF16)
    gk = carve(40 * KB, [2, S], BF16)
    vg = carve(48 * KB, [16, 384], BF16)
    pexp = carve(60 * KB, [2, 1024], BF16)
    tmpf = [carve((64 + 2 * i) * KB, [512], F32) for i in range(8)]
    tmpb = [carve((64 + 2 * i) * KB, [1024], BF16) for i in range(8)]
    merged = carve(40 * KB, [8, S], BF16)
    dtmp = [carve((72 + 2 * i) * KB, [512], F32) for i in range(3)]
    dsg = [carve((78 + i) * KB, [512], BF16) for i in range(2)]
    mixt = carve(0, [8, 512], F32)
    etmp = [carve((16 + 2 * i) * KB, [512], F32) for i in range(4)]
    esq = [carve((24 + i) * KB, [512], BF16) for i in range(4)]
    act = carve(0, [22, 1024], BF16)
    yg = [carve((44 + 8 * i) * KB, [1024], F32) for i in range(2)]
    yv = [carve((44 + 8 * i + 4) * KB, [1024], F32) for i in range(2)]
    geb = [carve((60 + 2 * i) * KB, [1024], BF16) for i in range(2)]
    halo = carve(64 * KB, [4, 4], F32)
    ffo = carve(44 * KB, [8, 1024], F32)
    fsq = [carve(76 * KB + 512 * i, [256], BF16) for i in range(4)]
    ftmp = [carve((78 + i) * KB, [256], F32) for i in range(4)]
    pTs = carve(0, [2, S], BF16)
    wple = carve(8 * KB, [2, 1024], BF16)
    ptmp = [carve((16 + 2 * i) * KB, [512], F32) for i in range(4)]

    def hk(kc, t):
        return ("h", kc, t)

    H_ALL = [hk(kc, t) for kc in range(8) for t in range(4)]

    bank_ptr = [0]

    def banks(n=1):
        b = bank_ptr[0]
        if b % n:
            b += n - (b % n)
        if b + n > 8:
            b = 0
        bank_ptr[0] = (b + n) % 8
        return list(range(b, b + n))

    def PK(b):
        return ("ps", b)

    attn_ctr = [0, 0]

    def acc_banks():
        i = attn_ctr[0]
        attn_ctr[0] += 1
        return [4, 5] if i % 2 == 0 else [6, 7]

    def sc_banks():
        i = attn_ctr[1]
        attn_ctr[1] += 1
        return [0, 1] if i % 2 == 0 else [2, 3]

    scr = P.sb("scr_sb", [128, 8], F32)
    ubound = P.sb("ubound_sb", [128, 44], F32)

    def bar():
        b = banks(1)[0]
        P.barrier({
            "pe": (E.matmul(ps[0:1, b, 0:1], ones_m[:, 0:1], ones_m[:, 0:1], start=True, stop=True), [PK(b), "cmr"]),
            "act": (E.activation(out=scr[:, 0:1], in_=scr[:, 1:2], func=AF.Copy), []),
            "dve": (E.tensor_copy(out=scr[:, 2:3], in_=scr[:, 3:4]), []),
            "pool": (E.memset(scr[:, 4:5], 0.0), []),
        })

    class Stream:
        def __init__(self):
            self.seq = []
            self.issued = 0
            self.cur = -1

        def plan(self, seq):
            self.seq = seq

        def _issue(self, n):
            l, nm = self.seq[n]
            o, w = offs[nm]
            slot = n % NSLOT
            src = ws_d[:, l * LW + o: l * LW + o + w]
            P.dma("pool", E.dma_start(out=ring[:, slot, 0:w], in_=src),
                  writes=[("ring", slot)])

        def use(self, l, nm):
            self.cur += 1
            n = self.cur
            assert self.seq[n] == (l, nm), (self.seq[n], l, nm)
            while self.issued < min(n + 1 + PREFETCH, len(self.seq)):
                self._issue(self.issued)
                self.issued += 1
            slot = n % NSLOT
            w = offs[nm][1]
            return ring[:, slot, 0:w], ("ring", slot)

    stream = Stream()
    seq = []
    for l in range(nl):
        for nm in ("ag", "av", "dqa", "dqb", "dkb", "vd", "gqa", "gqb", "gk", "vg"):
            seq.append((l, nm))
        for j in range(8):
            seq.append((l, "mgA%d" % j))
            seq.append((l, "mgB%d" % j))
        for t in range(TT):
            for i in range(4):
                seq.append((l, "wo%d" % i))
        for hf in range(2):
            for i in range(22):
                seq.append((l, "up%d" % i))
            for j in range(8):
                seq.append((l, "dn%da" % j))
                seq.append((l, "dn%db" % j))
        for i in range(4):
            seq.append((l, "pg%d" % i))
    stream.plan(seq)

    def proj_fm(bank, wap, wkey, rhs_fn, rkeys_fn, nk=8):
        for kc in range(nk):
            P.op("pe", E.matmul(ps[:, bank, :], wap[:, kc, :], rhs_fn(kc),
                                                 start=(kc == 0), stop=(kc == nk - 1)),
                 reads=[wkey] + rkeys_fn(kc), writes=[PK(bank)])

    def rstd_from_bank(bank, dst, dkey, scale, bias, width=512):
        P.op("act", E.activation(out=dst, in_=ps[:, bank, 0:width], func=AF.Ln, scale=scale, bias=bias),
             reads=[PK(bank)], writes=[dkey])
        P.op("act", E.activation(out=dst, in_=dst, func=AF.Exp, scale=-0.5),
             reads=[dkey], writes=[dkey])

    def pcol(l, nm, i=0):
        o, w = play[nm]
        return prm[:, l * NPRM + o + i: l * NPRM + o + i + 1]

    def norm_to_h(l, gname, tmp_pool, sq_pool, tkey):
        for t in range(TT):
            sl = slice(t * 512, (t + 1) * 512)
            b = banks(1)[0]
            for c in range(8):
                sq = sq_pool[c % len(sq_pool)][:, 0:512]
                sk = (tkey, "sq", c % len(sq_pool))
                P.op("act", E.activation(out=sq, in_=xT[:, c, sl], func=AF.Square),
                     reads=[("x", c, t)], writes=[sk])
                P.op("pe", E.matmul(ps[:, b, :], ones_m, sq, start=(c == 0), stop=(c == 7)),
                     reads=[sk, "cm"], writes=[PK(b)])
            rs = tmp_pool[t % 2]
            rk = (tkey, "rs", t % 2)
            rstd_from_bank(b, rs, rk, 1.0 / D, EPS)
            for c in range(8):
                P.op("dve", E.scalar_tensor_tensor(
                    out=hT[:, c, sl], in0=xT[:, c, sl], scalar=pcol(l, gname, c), in1=rs,
                    op0=ALU.mult, op1=ALU.mult),
                    reads=[("x", c, t), rk, "prm"], writes=[hk(c, t)])

    def post_norm_add(l, gname, src_fn, skey_fn, width, ntile, tok0_fn, sq_pool, tmp_pool, tkey, xt_fn):
        for i in range(ntile):
            b = banks(1)[0]
            for c in range(8):
                sq = sq_pool[c % len(sq_pool)]
                sk = (tkey, "sq", c % len(sq_pool))
                P.op("act", E.activation(out=sq[:, 0:width], in_=src_fn(c, i), func=AF.Square),
                     reads=[skey_fn(c, i)], writes=[sk])
                P.op("pe", E.matmul(ps[:, b, 0:width], ones_m, sq[:, 0:width],
                                                          start=(c == 0), stop=(c == 7)),
                     reads=[sk, "cm"], writes=[PK(b)])
            rs = tmp_pool[i % 2]
            rk = (tkey, "rs", i % 2)
            rstd_from_bank(b, rs[:, 0:width], rk, 1.0 / D, EPS, width=width)
            for c in range(8):
                tk = (tkey, "t", c % 2)
                tb = tmp_pool[2 + c % 2]
                xs, xkeys = xt_fn(c, i)
                P.op("dve", E.scalar_tensor_tensor(
                    out=tb[:, 0:width], in0=src_fn(c, i), scalar=pcol(l, gname, c), in1=rs[:, 0:width],
                    op0=ALU.mult, op1=ALU.mult),
                    reads=[skey_fn(c, i), rk, "prm"], writes=[tk])
                P.op("dve", E.tensor_tensor(out=xs, in0=xs, in1=tb[:, 0:width], op=ALU.add),
                     reads=[tk] + xkeys, writes=xkeys)

    P.op("dve", E.memset(scr[:], 0.0), writes=["scr0"])
    P.dma("sp", E.dma_start(out=prm[:], in_=prm_d), writes=["prm"])
    P.dma("pool", E.dma_start(out=cm[:], in_=cm_d), writes=["cm"])
    for c in range(8):
        P.dma("sp", E.dma_start(out=xT[:, c, :], in_=xT_d[c]),
              writes=[("x", c, t) for t in range(4)], slot=("xload", c))
    lam_inits = [0.8 - 0.6 * math.exp(-0.3 * l) for l in range(L)]
    for l in range(nl):
        o, w = play["dl"]
        dl = prm[:, l * NPRM + o: l * NPRM + o + 128]
        pr = tmpf[0]
        P.op("dve", E.tensor_tensor(
            out=pr[:, 0:64].rearrange("p (a b) -> p a b", a=2),
            in0=dl.rearrange("p (a b c) -> p a b c", a=2, b=2)[:, :, 0, :],
            in1=dl.rearrange("p (a b c) -> p a b c", a=2, b=2)[:, :, 1, :], op=ALU.mult),
            reads=["prm"], writes=["lamtmp"])
        P.op("dve", E.tensor_reduce(
            out=pr[:, 64:66], in_=pr[:, 0:64].rearrange("p (a b) -> p a b", a=2), axis=AX.X, op=ALU.add),
            reads=["lamtmp"], writes=["lamtmp2"])
        P.op("act", E.activation(out=pr[:, 66:68], in_=pr[:, 64:66], func=AF.Exp),
             reads=["lamtmp2"], writes=["lamtmp3"])
        P.op("dve", E.scalar_tensor_tensor(
            out=lamt[:, 4 * l: 4 * l + 1], in0=pr[:, 66:67], scalar=lam_inits[l], in1=pr[:, 67:68],
            op0=ALU.add, op1=ALU.subtract), reads=["lamtmp3"], writes=[("lam", l)])
        P.op("dve", E.tensor_scalar(
            out=lamt[:, 4 * l + 1: 4 * l + 2], in0=lamt[:, 4 * l: 4 * l + 1], scalar1=-1.0, scalar2=None,
            op0=ALU.mult), reads=[("lam", l)], writes=[("nlam", l)])

    dumped = [False]

    def dump(name, items):
        if dbg != name or dumped[0]:
            return False
        col = 0
        stage = tmpf[7]
        for ap, n, keys, is_f32 in items:
            for s0 in range(0, n, 512):
                w = min(512, n - s0)
                if is_f32:
                    src = ap[:, s0:s0 + w]
                    P.dma("sp", E.dma_start(out=dbg_d[:, col:col + w], in_=src),
                          reads=keys, writes=[("dbg", col)], final=True)
                else:
                    P.op("dve", E.tensor_copy(out=stage[:, 0:w], in_=ap[:, s0:s0 + w]),
                         reads=keys, writes=["dbgstage"])
                    P.dma("sp", E.dma_start(out=dbg_d[:, col:col + w], in_=stage[:, 0:w]),
                          reads=["dbgstage"], writes=[("dbg", col)], final=True)
                col += w
        dumped[0] = True
        return True

    P.op("pe", E.matmul(ps[0:1, 0, 0:1], ones_m[:, 0:1], ones_m[:, 0:1], start=True, stop=True),
         reads=["cm"], writes=[PK(0), "cmr"])
    bar()
    for l in range(nl):
        norm_to_h(l, "nmp", tmpf[0:2], tmpb[2:6], "n0")
        bar()
        if dump("h", [(hT[:, c, :], S, [hk(c, t) for t in range(4)], False) for c in range(8)]):
            break

        P.op("pool", E.memset(upad[:, :, 0:15], 0.0), writes=[("upadL",)])
        P.op("pool", E.memset(upad[:, :, 15 + S: 32 + S], 0.0), writes=[("upadR",)])
        for c in range(2):
            for k in range(CONV_W):
                P.op("dve", E.tensor_scalar(
                    out=dg[:, c, k, :], in0=ident_m, scalar1=pcol(l, "cw", c * 31 + k), scalar2=None, op0=ALU.mult),
                    reads=["cm", "prm"], writes=[("dg", c)])
        wag, kag = stream.use(l, "ag")
        wag = wag.rearrange("p (u k c) -> p u k c", u=2, k=8)
        sg = [tmpb[0], tmpb[1]]
        sgt = {}
        for c in range(2):
            for t in range(TT):
                b = banks(1)[0]
                proj_fm(b, wag[:, c], kag, lambda kc, t=t: hT[:, kc, t * 512:(t + 1) * 512], lambda kc, t=t: [hk(kc, t)])
                dst = tmpb[c * 2 + t // 2][:, (t % 2) * 512:(t % 2) * 512 + 512]
                key = ("sg", c, t)
                sgt[(c, t)] = (dst, key)
                P.op("act", E.activation(out=dst, in_=ps[:, b, :], func=AF.Sigmoid),
                     reads=[PK(b)], writes=[key])
        wav, kav = stream.use(l, "av")
        wav = wav.rearrange("p (u k c) -> p u k c", u=2, k=8)
        for c in range(2):
            for t in range(TT):
                b = banks(1)[0]
                proj_fm(b, wav[:, c], kav, lambda kc, t=t: hT[:, kc, t * 512:(t + 1) * 512], lambda kc, t=t: [hk(kc, t)])
                dst, key = sgt[(c, t)]
                P.op("dve", E.tensor_tensor(
                    out=upad[:, c, 15 + t * 512: 15 + (t + 1) * 512], in0=ps[:, b, :], in1=dst, op=ALU.mult),
                    reads=[PK(b), key], writes=[("upad", c, t)])
        yc = [carve((41 + 2 * i) * KB, [512], F32) for i in range(2)]
        ybf = [carve((45 + i) * KB, [512], BF16) for i in range(2)]
        ysq = [carve((47 + i) * KB, [512], BF16) for i in range(2)]
        mu = tmpf[4]
        va = tmpf[5]
        zt = [tmpf[6], tmpf[7]]
        for t in range(TT):
            cb_ = []
            for c in range(2):
                b = banks(1)[0]
                cb_.append(b)
                for k in range(CONV_W):
                    lo = t * 512 + k
                    rk = [("upad", c, tt_) for tt_ in range(max(0, t - 1), min(TT, t + 2))] + [("upadL",), ("upadR",)]
                    P.op("pe", E.matmul(
                        ps[:, b, :], dg[:, c, k, :], upad[:, c, lo: lo + 512], start=(k == 0), stop=(k == CONV_W - 1)),
                        reads=[("dg", c)] + rk, writes=[PK(b)])
                P.op("act", E.activation(out=yc[c], in_=ps[:, b, :], func=AF.Identity,
                                                             bias=pcol(l, "cb", c)),
                     reads=[PK(b), "prm"], writes=[("yc", c)])
                P.op("act", E.activation(out=ybf[c], in_=ps[:, b, :], func=AF.Identity,
                                                             bias=pcol(l, "cb", c)),
                     reads=[PK(b), "prm"], writes=[("ybf", c)])
                P.op("act", E.activation(out=ysq[c], in_=ps[:, b, :], func=AF.Square,
                                                             bias=pcol(l, "cb", c)),
                     reads=[PK(b), "prm"], writes=[("ysq", c)])
            b1 = banks(1)[0]
            b2 = banks(1)[0]
            for c in range(2):
                P.op("pe", E.matmul(ps[:, b1, :], ones_m, ybf[c], start=(c == 0), stop=(c == 1)),
                     reads=[("ybf", c), "cm"], writes=[PK(b1)])
            for c in range(2):
                P.op("pe", E.matmul(ps[:, b2, :], ones_m, ysq[c], start=(c == 0), stop=(c == 1)),
                     reads=[("ysq", c), "cm"], writes=[PK(b2)])
            P.op("dve", E.tensor_scalar(out=mu, in0=ps[:, b1, :], scalar1=1.0 / CONV_CH, scalar2=None, op0=ALU.mult),
                 reads=[PK(b1)], writes=["mu"])
            P.op("dve", E.tensor_tensor(out=va, in0=mu, in1=mu, op=ALU.mult), reads=["mu"], writes=["va"])
            P.op("dve", E.scalar_tensor_tensor(out=va, in0=ps[:, b2, :], scalar=1.0 / CONV_CH, in1=va,
                                                         op0=ALU.mult, op1=ALU.subtract),
                 reads=[PK(b2), "va"], writes=["va"])
            P.op("act", E.activation(out=va, in_=va, func=AF.Ln, bias=EPS), reads=["va"], writes=["va"])
            P.op("act", E.activation(out=va, in_=va, func=AF.Exp, scale=-0.5), reads=["va"], writes=["va"])
            for c in range(2):
                z = zt[c]
                P.op("dve", E.tensor_tensor(out=z, in0=yc[c], in1=mu, op=ALU.subtract),
                     reads=[("yc", c), "mu"], writes=[("z", c)])
                P.op("dve", E.tensor_tensor(out=z, in0=z, in1=va, op=ALU.mult),
                     reads=[("z", c), "va"], writes=[("z", c)])
                P.op("dve", E.tensor_scalar(out=z, in0=z, scalar1=pcol(l, "clg", c),
                                                                scalar2=pcol(l, "clb", c), op0=ALU.mult, op1=ALU.add),
                     reads=[("z", c), "prm"], writes=[("z", c)])
                P.op("act", E.activation(out=uA[:, c, t * 512:(t + 1) * 512], in_=z, func=AF.Silu),
                     reads=[("z", c)], writes=[("uA", c, t)])
        bar()
        if dump("uA", [(uA[:, c, :], S, [("uA", c, t) for t in range(4)], False) for c in range(2)]):
            break

        P.dma("pool", E.dma_start(out=tab[:, 0, :], in_=rope_d[:, 0:S]),
              reads=[("upad", c, t) for c in range(2) for t in range(4)] + [("dg", 0), ("dg", 1)],
              writes=[("tab", 0)])
        P.dma("pool", E.dma_start(out=tab[:, 1, :], in_=rope_d[:, S:2 * S]),
              reads=[("upad", c, t) for c in range(2) for t in range(4)] + [("dg", 0), ("dg", 1)],
              writes=[("tab", 1)])
        P.op("pool", E.memset(vd[:, :, 64:128], 1.0), reads=[("dg", 0), ("dg", 1)], writes=[("vd1",)])
        P.op("pool", E.memset(vd[:, :, 256:320], 1.0), reads=[("dg", 0), ("dg", 1)], writes=[("vd1b",)])

        rope_ctr = [0]

        def lazy_tile(name):
            cache = []

            def get():
                if not cache:
                    w_, k_ = stream.use(l, name)
                    cache.append((w_.rearrange("p (u k c) -> p u k c", u=2, k=8), k_))
                return cache[0]
            return get

        def rope_stages(wget, uidx, dst, dchunk, dname, rmat, norm=None):
            stages = []
            for t in range(TT):
                st = {}

                def s1(t=t, st=st):
                    wslot, wkey = wget()
                    n = rope_ctr[0]
                    rope_ctr[0] += 1
                    par = n % 2
                    st["par"] = par
                    bA = banks(1)[0]
                    st["bA"] = bA
                    proj_fm(bA, wslot[:, uidx], wkey, lambda kc, t=t: hT[:, kc, t * 512:(t + 1) * 512], lambda kc, t=t: [hk(kc, t)])
                    qraw = tmpb[par][:, 0:512]
                    qk = ("qraw", par)
                    if norm is None:
                        P.op("act", E.activation(out=qraw, in_=ps[:, bA, :], func=AF.Copy),
                             reads=[PK(bA)], writes=[qk])
                    else:
                        P.op("act", E.activation(out=qraw, in_=ps[:, bA, :], func=AF.Copy, scale=pcol(l, norm)),
                             reads=[PK(bA), "prm"], writes=[qk])
                        sq = tmpb[par][:, 512:1024]
                        P.op("act", E.activation(out=sq, in_=ps[:, bA, :], func=AF.Square),
                             reads=[PK(bA)], writes=[("qsq", par)])

                def s2(t=t, st=st):
                    par = st["par"]
                    bA = st["bA"]
                    sl = slice(t * 512, (t + 1) * 512)
                    qraw = tmpb[par][:, 0:512]
                    qk = ("qraw", par)
                    sq = tmpb[par][:, 512:1024]
                    sk = ("qsq", par)
                    bB = banks(1)[0]
                    P.op("pe", E.matmul(ps[:, bB, :], rmat, qraw, start=True, stop=True),
                         reads=[qk, "cm"], writes=[PK(bB)])
                    if norm is not None:
                        bC = banks(1)[0]
                        P.op("pe", E.matmul(ps[:, bC, :], blk64_m, sq, start=True, stop=True),
                             reads=[sk, "cm"], writes=[PK(bC)])
                    t1 = tmpf[2 + par]
                    t2 = tmpf[4 + par]
                    k1 = ("t1", par)
                    k2 = ("t2", par)
                    if norm is None:
                        P.op("dve", E.tensor_tensor(out=t1, in0=ps[:, bA, :], in1=tab[:, 0, sl], op=ALU.mult),
                             reads=[PK(bA), ("tab", 0)], writes=[k1])
                    else:
                        P.op("dve", E.scalar_tensor_tensor(
                            out=t1, in0=ps[:, bA, :], scalar=pcol(l, norm), in1=tab[:, 0, sl], op0=ALU.mult, op1=ALU.mult),
                            reads=[PK(bA), ("tab", 0), "prm"], writes=[k1])
                    P.op("dve", E.tensor_tensor(out=t2, in0=ps[:, bB, :], in1=tab[:, 1, sl], op=ALU.mult),
                         reads=[PK(bB), ("tab", 1)], writes=[k2])
                    if norm is None:
                        P.op("dve", E.tensor_tensor(out=dst[:, dchunk, sl], in0=t1, in1=t2, op=ALU.add),
                             reads=[k1, k2], writes=[(dname, dchunk, t)])
                    else:
                        rs = tmpf[6 + par]
                        rk = ("qrs", par)
                        rstd_from_bank(bC, rs, rk, 1.0 / 64, EPS)
                        P.op("dve", E.tensor_tensor(out=t1, in0=t1, in1=t2, op=ALU.add),
                             reads=[k1, k2], writes=[k1])
                        P.op("dve", E.tensor_tensor(out=dst[:, dchunk, sl], in0=t1, in1=rs, op=ALU.mult),
                             reads=[k1, rk], writes=[(dname, dchunk, t)])
                stages.append((s1, s2))
            return stages

        def run_stages(stages):
            stages[0][0]()
            for i_ in range(len(stages)):
                if i_ + 1 < len(stages):
                    stages[i_ + 1][0]()
                stages[i_][1]()

        g_ = lazy_tile("dqa")
        stg = rope_stages(g_, 0, qd, 0, "qd", r32_m) + rope_stages(g_, 1, qd, 1, "qd", r32_m)
        g_ = lazy_tile("dqb")
        stg += rope_stages(g_, 0, kd, 0, "kd", r32_m) + rope_stages(g_, 1, kd, 1, "kd", r32_m)
        g_ = lazy_tile("dkb")
        stg += rope_stages(g_, 0, kd, 2, "kd", r32_m) + rope_stages(g_, 1, kd, 3, "kd", r32_m)
        run_stages(stg)
        wv, kv = stream.use(l, "vd")
        wv = wv.rearrange("p (k c) -> p k c", k=8)
        for kt in range(16):
            b = banks(1)[0]
            for kc in range(8):
                P.op("pe", E.matmul(ps[:, b, 0:256], hT[:, kc, kt * 128:(kt + 1) * 128], wv[:, kc, :],
                                                                 start=(kc == 0), stop=(kc == 7)),
                     reads=[kv, hk(kc, kt // 4)], writes=[PK(b)])
            P.op("act", E.activation(
                out=vd[:, kt, 0:256].rearrange("p (a b) -> p a b", a=2)[:, :, 0:64],
                in_=ps[:, b, 0:128].rearrange("p (a b) -> p a b", a=2), func=AF.Copy),
                reads=[PK(b)], writes=[("vd", kt, 0)])
            P.op("act", E.activation(
                out=vd[:, kt, 192:448 - 64].rearrange("p (a b) -> p a b", b=64)[:, 0:3:2, :],
                in_=ps[:, b, 128:256].rearrange("p (a b) -> p a b", a=2), func=AF.Copy),
                reads=[PK(b)], writes=[("vd", kt, 1)])
        bar()
        if dump("qd", [(qd[:, c, :], S, [("qd", c, t) for t in range(4)], False) for c in range(2)] +
                [(kd[:, c, :], S, [("kd", c, t) for t in range(4)], False) for c in range(4)]):
            break

        pexp_l = [pexp[:, 0, :], pexp[:, 1, :], tmpb[7]]
        step_ctr = [0]

        def run_attn(units):
            steps = [(u, kc) for u in units for kc in range(16)]
            state = {}

            def emit_S(i):
                u, kc = steps[i]
                if kc == 0:
                    u["acc"] = acc_banks()
                sb_ = sc_banks()
                for s_ in range(2):
                    kap, kkeys = u["k_fn"](s_, kc)
                    qap, qkeys = u["q"][s_]
                    P.op("pe", E.matmul(ps[:, sb_[s_], :], kap, qap, start=True, stop=True),
                         reads=kkeys + qkeys, writes=[PK(sb_[s_])])
                state[i] = sb_

            def emit_X(i):
                u, kc = steps[i]
                sb_ = state.pop(i)
                n = step_ctr[0]
                step_ctr[0] += 1
                pe_ = pexp_l[n % 3]
                pk_ = ("pexp", n % 3)
                P.op("act", E.activation(out=pe_.rearrange("p (a b) -> p a b", a=2), in_=ps[:, sb_[0]:sb_[0] + 2, :],
                                         func=AF.Exp, scale=u["scale"]),
                     reads=[PK(sb_[0]), PK(sb_[1])], writes=[pk_])
                state[("x", i)] = (pe_, pk_)

            def emit_V(i):
                u, kc = steps[i]
                pe_, pk_ = state.pop(("x", i))
                acc = u["acc"]
                for s_ in range(2):
                    vap, vkeys = u["v_fn"](s_, kc)
                    P.op("pe", E.matmul(ps[:, acc[s_], :], vap, pe_[:, s_ * 512:(s_ + 1) * 512],
                                        start=(kc == 0), stop=(kc == 15)),
                         reads=vkeys + [pk_], writes=[PK(acc[s_])])
                if kc == 15:
                    cont = u["post"](acc)
                    if cont is not None:
                        pending.append((i + 12, cont))

            pending = []
            n_ = len(steps)
            emit_S(0)
            if n_ > 1:
                emit_S(1)
            for i in range(n_):
                emit_X(i)
                if i + 2 < n_:
                    emit_S(i + 2)
                emit_V(i)
                while pending and pending[0][0] <= i:
                    pending.pop(0)[1]()
            while pending:
                pending.pop(0)[1]()

        o12 = [tmpf[3], tmpf[4]]
        odf = [tmpf[0], tmpf[1]]
        units = []
        for pair in range(2):
            for qt in range(TT):
                for sm in range(2):
                    def k_fn(hh, kc, pair=pair, sm=sm):
                        return kd[64 * hh:64 * hh + 64, 2 * sm + pair, kc * 128:(kc + 1) * 128], [("kd", 2 * sm + pair, kc // 4)]

                    def v_fn(hh, kc, pair=pair):
                        vlo = pair * 192 + (0 if hh == 0 else 64)
                        return vd[:, kc, vlo:vlo + 128], [("vd", kc, pair), ("vd1",), ("vd1b",)]

                    def post(acc, pair=pair, qt=qt, sm=sm):
                        rd = tmpf[2]
                        o_ = o12[sm]
                        P.op("dve", E.reciprocal(out=rd[64:128, :], in_=ps[64:128, acc[0], :]),
                             reads=[PK(acc[0])], writes=[("rd", 0)])
                        P.op("dve", E.reciprocal(out=rd[0:64, :], in_=ps[0:64, acc[1], :]),
                             reads=[PK(acc[1])], writes=[("rd", 1)])
                        P.op("dve", E.tensor_tensor(out=o_[0:64, :], in0=ps[0:64, acc[0], :], in1=rd[64:128, :], op=ALU.mult),
                             reads=[PK(acc[0]), ("rd", 0)], writes=[("o12", sm, 0)])
                        P.op("dve", E.tensor_tensor(out=o_[64:128, :], in0=ps[64:128, acc[1], :], in1=rd[0:64, :], op=ALU.mult),
                             reads=[PK(acc[1]), ("rd", 1)], writes=[("o12", sm, 1)])
                        if sm == 0:
                            return None
                        od = odf[qt % 2]
                        P.op("dve", E.scalar_tensor_tensor(
                            out=od, in0=o12[1], scalar=lamt[:, 4 * l + 1: 4 * l + 2], in1=o12[0],
                            op0=ALU.mult, op1=ALU.add),
                            reads=[("o12", 0, 0), ("o12", 0, 1), ("o12", 1, 0), ("o12", 1, 1), ("nlam", l)],
                            writes=[("od", qt % 2)])
                        osq = tmpb[5][:, 0:512]

                        def cont(acc=acc, od=od, pair=pair, qt=qt, osq=osq):
                            P.op("act", E.activation(out=osq, in_=od, func=AF.Square),
                                 reads=[("od", qt % 2)], writes=["osq"])
                            b = acc[0]
                            P.op("pe", E.matmul(ps[:, b, :], blk64_m, osq, start=True, stop=True),
                                 reads=["osq", "cm"], writes=[PK(b)])
                            rs = tmpf[6]
                            li = lam_inits[l]
                            rstd_from_bank(b, rs, "ors", 1.0 / (64 * (1 - li) ** 2), EPS / (1 - li) ** 2)
                            P.op("dve", E.scalar_tensor_tensor(
                                out=oB[:, pair, qt * 512:(qt + 1) * 512], in0=od, scalar=pcol(l, "subg"), in1=rs,
                                op0=ALU.mult, op1=ALU.mult),
                                reads=[("od", qt % 2), "ors", "prm"], writes=[("oB", pair, qt)])
                        return cont

                    q = [(qd[64 * hh:64 * hh + 64, pair, qt * 512:(qt + 1) * 512], [("qd", pair, qt)]) for hh in range(2)]
                    units.append(dict(q=q, k_fn=k_fn, v_fn=v_fn, scale=32 ** -0.5, post=post))
        run_attn(units)
        bar()
        if dump("oB", [(oB[:, c, :], S, [("oB", c, t) for t in range(4)], False) for c in range(2)]):
            break

        QD_KD = [("qd", c, t) for c in range(2) for t in range(4)] + [("kd", c, t) for c in range(4) for t in range(4)]
        VD_ALL = [("vd", kt, i) for kt in range(16) for i in range(2)] + [("vd1",), ("vd1b",)]
        P.dma("pool", E.dma_start(out=tab[:, 0, :], in_=rope_d[:, 2 * S:3 * S]), writes=[("tab", 0)])
        P.dma("pool", E.dma_start(out=tab[:, 1, :], in_=rope_d[:, 3 * S:4 * S]), writes=[("tab", 1)])
        P.op("pool", E.memset(vg[:, :, 0:64], 1.0), reads=VD_ALL, writes=[("vg1", 0)])
        P.op("pool", E.memset(vg[:, :, 128:256], 1.0), reads=VD_ALL, writes=[("vg1", 1)])
        P.op("pool", E.memset(vg[:, :, 320:384], 1.0), reads=VD_ALL, writes=[("vg1", 2)])
        VG1 = [("vg1", i) for i in range(3)]

        def gq_alias_guard(keys):
            return keys

        g_ = lazy_tile("gqa")
        stg = rope_stages(g_, 0, gq, 0, "gq", r64_m, norm="gqn") + rope_stages(g_, 1, gq, 1, "gq", r64_m, norm="gqn")
        g_ = lazy_tile("gqb")
        stg += rope_stages(g_, 0, gq, 2, "gq", r64_m, norm="gqn") + rope_stages(g_, 1, gq, 3, "gq", r64_m, norm="gqn")
        g_ = lazy_tile("gk")
        stg += rope_stages(g_, 0, gk, 0, "gk", r64_m, norm="gkn") + rope_stages(g_, 1, gk, 1, "gk", r64_m, norm="gkn")
        run_stages(stg)
        wv, kv = stream.use(l, "vg")
        wv = wv.rearrange("p (k c) -> p k c", k=8)
        for kt in range(16):
            b = banks(1)[0]
            for kc in range(8):
                P.op("pe", E.matmul(ps[:, b, 0:128], hT[:, kc, kt * 128:(kt + 1) * 128], wv[:, kc, :],
                                                                 start=(kc == 0), stop=(kc == 7)),
                     reads=[kv, hk(kc, kt // 4)], writes=[PK(b)])
            P.op("act", E.activation(
                out=vg[:, kt, :].rearrange("p (a b) -> p a b", a=2)[:, :, 64:128],
                in_=ps[:, b, 0:128].rearrange("p (a b) -> p a b", a=2), func=AF.Copy),
                reads=[PK(b)] + VD_ALL, writes=[("vg", kt)])
        bar()
        if dump("gq", [(gq[:, c, :], S, [("gq", c, t) for t in range(4)], False) for c in range(4)] +
                [(gk[:, c, :], S, [("gk", c, t) for t in range(4)], False) for c in range(2)]):
            break
        units = []
        for c in range(4):
            g = c // 2
            for qt in range(TT):
                def k_fn(s_, kc, g=g):
                    return gk[64 * s_:64 * s_ + 64, g, kc * 128:(kc + 1) * 128], [("gk", g, kc // 4)]

                def v_fn(s_, kc, g=g):
                    vlo = g * 192 + (64 if s_ == 0 else 0)
                    return vg[:, kc, vlo:vlo + 128], [("vg", kc)] + VG1

                def post(acc, c=c, qt=qt):
                    rd = tmpf[2]
                    P.op("dve", E.reciprocal(out=rd[64:128, :], in_=ps[64:128, acc[0], :]),
                         reads=[PK(acc[0])], writes=[("rdg", 0)])
                    P.op("dve", E.reciprocal(out=rd[0:64, :], in_=ps[0:64, acc[1], :]),
                         reads=[PK(acc[1])], writes=[("rdg", 1)])
                    P.op("dve", E.tensor_tensor(
                        out=gq[0:64, c, qt * 512:(qt + 1) * 512], in0=ps[0:64, acc[0], :], in1=rd[64:128, :], op=ALU.mult),
                        reads=[PK(acc[0]), ("rdg", 0)], writes=[("gq", c, qt)])
                    P.op("dve", E.tensor_tensor(
                        out=gq[64:128, c, qt * 512:(qt + 1) * 512], in0=ps[64:128, acc[1], :], in1=rd[0:64, :], op=ALU.mult),
                        reads=[PK(acc[1]), ("rdg", 1), ("gq", c, qt)], writes=[("gq", c, qt)])

                q = [(gq[64 * s_:64 * s_ + 64, c, qt * 512:(qt + 1) * 512], [("gq", c, qt)]) for s_ in range(2)]
                units.append(dict(q=q, k_fn=k_fn, v_fn=v_fn, scale=64 ** -0.5, post=post))
        run_attn(units)
        oC = gq
        bar()
        if dump("oC", [(oC[:, c, :], S, [("gq", c, t) for t in range(4)], False) for c in range(4)]):
            break

        GKV = [("gk", c, t) for c in range(2) for t in range(4)] + [("vg", kt) for kt in range(16)] + VG1 + \
              [("pexp", 0), ("pexp", 1)]
        first_merge = [True]
        for j in range(8):
            wA, kA = stream.use(l, "mgA%d" % j)
            wA = wA.rearrange("p (u k c) -> p u k c", u=2, k=8)
            wB, kB = stream.use(l, "mgB%d" % j)
            wB = wB.rearrange("p (u k c) -> p u k c", u=2, k=8)
            gw = [(wA[:, 0], kA), (wA[:, 1], kA), (wB[:, 0], kB)]
            ow = wB[:, 1]
            srcs = [(uA, "uA", 2, 0), (oB, "oB", 2, 2), (oC, "gq", 4, 4)]
            for t in range(TT):
                sl = slice(t * 512, (t + 1) * 512)
                m = dtmp[0 + (t % 2)]
                mk_ = ("m", t % 2)
                for x in range(3):
                    bG = banks(1)[0]
                    proj_fm(bG, gw[x][0], gw[x][1], lambda kc, t=t: hT[:, kc, t * 512:(t + 1) * 512], lambda kc, t=t: [hk(kc, t)])
                    src, sname, nk, k0 = srcs[x]
                    bO = banks(1)[0]
                    for kc in range(nk):
                        P.op("pe", E.matmul(
                            ps[:, bO, :], ow[:, k0 + kc, :], src[:, kc, sl], start=(kc == 0), stop=(kc == nk - 1)),
                            reads=[kB, (sname, kc, t)], writes=[PK(bO)])
                    sgx = dsg[x % 2]
                    sgk = ("dsg", x % 2)
                    extra = GKV if first_merge[0] else []
                    P.op("act", E.activation(out=sgx, in_=ps[:, bG, :], func=AF.Sigmoid),
                         reads=[PK(bG)], writes=[sgk])
                    if x == 0:
                        P.op("dve", E.tensor_tensor(out=m, in0=ps[:, bO, :], in1=sgx, op=ALU.mult),
                             reads=[PK(bO), sgk], writes=[mk_])
                    else:
                        tb = dtmp[2]
                        P.op("dve", E.tensor_tensor(out=tb, in0=ps[:, bO, :], in1=sgx, op=ALU.mult),
                             reads=[PK(bO), sgk], writes=["dtb"])
                        if x == 1:
                            P.op("dve", E.tensor_tensor(out=m, in0=m, in1=tb, op=ALU.add),
                                 reads=[mk_, "dtb"], writes=[mk_])
                        else:
                            P.op("dve", E.tensor_tensor(out=merged[:, j, sl], in0=m, in1=tb, op=ALU.add),
                                 reads=[mk_, "dtb"] + extra, writes=[("mg", j, t)])
                            first_merge[0] = False
        bar()
        if dump("merged", [(merged[:, c, :], S, [("mg", c, t) for t in range(4)], False) for c in range(8)]):
            break

        mixbuf = [[mixt[:, jo, :] for jo in range(8)],
                  [carve((28 + 2 * jo) * KB, [512], F32) for jo in range(6)] + [carve((72 + 2 * jo) * KB, [512], F32) for jo in range(2)]]

        def wout_tile(t):
            sl = slice(t * 512, (t + 1) * 512)
            for i4 in range(4):
                wo_, kwo = stream.use(l, "wo%d" % i4)
                wo_ = wo_.rearrange("p (u k c) -> p u k c", u=2, k=8)
                for u_ in range(2):
                    jo = 2 * i4 + u_
                    b = banks(1)[0]
                    for j in range(8):
                        P.op("pe", E.matmul(ps[:, b, :], wo_[:, u_, j, :], merged[:, j, sl],
                                            start=(j == 0), stop=(j == 7)),
                             reads=[kwo, ("mg", j, t)], writes=[PK(b)])
                    P.op("act", E.activation(out=mixbuf[t % 2][jo], in_=ps[:, b, :], func=AF.Copy),
                         reads=[PK(b)], writes=[("mix", t % 2, jo)])

        wout_tile(0)
        for t in range(TT):
            sl = slice(t * 512, (t + 1) * 512)
            if t + 1 < TT:
                wout_tile(t + 1)
            post_norm_add(l, "nmpost", lambda c, i, t=t: mixbuf[t % 2][c], lambda c, i, t=t: ("mix", t % 2, c), 512, 1,
                          None, esq, etmp, "e", lambda c, i, t=t, sl=sl: (xT[:, c, sl], [("x", c, t)]))
        bar()
        if dump("x1", [(xT[:, c, :], S, [("x", c, t) for t in range(4)], True) for c in range(8)]):
            break

        norm_to_h(l, "nfp", etmp[0:2], esq, "n1")
        if dbg == "h2":
            bar()
            dump("h2", [(hT[:, c, :], S, [hk(c, t) for t in range(4)], False) for c in range(8)])
            break
        for hf in range(2):
            tok0 = hf * 1024
            bar()
            for i in range(22):
                wu_, ku = stream.use(l, "up%d" % i)
                wu_ = wu_.rearrange("p (u k c) -> p u k c", u=2, k=8)
                ys = []
                for which in range(2):
                    ch = i + 22 * which
                    bb = banks(2)
                    for tt_ in range(2):
                        t = hf * 2 + tt_
                        proj_fm(bb[tt_], wu_[:, which], ku, lambda kc, t=t: hT[:, kc, t * 512:(t + 1) * 512],
                                lambda kc, t=t: [hk(kc, t)])
                    bh = None
                    if hf == 0:
                        bh = banks(1)[0]
                        for kc in range(8):
                            P.op("pe", E.matmul(ps[:, bh, 0:1], wu_[:, which, kc, :], hT[:, kc, 1024:1025],
                                                start=(kc == 0), stop=(kc == 7)),
                                 reads=[ku, hk(kc, 2)], writes=[PK(bh)])
                    y = (yg if which == 0 else yv)[i % 2]
                    yk = ("y", which, i % 2)
                    w0 = pcol(l, "fw", 0 * 44 + ch)
                    w1 = pcol(l, "fw", 1 * 44 + ch)
                    w2 = pcol(l, "fw", 2 * 44 + ch)
                    bcol = pcol(l, "fb", ch)
                    pv = ps[:, bb[0]:bb[0] + 2, :].rearrange("p a b -> p (a b)")
                    extra = []
                    P.op("act", E.activation(out=y, in_=pv, func=AF.Identity, scale=w1, bias=bcol),
                         reads=[PK(bb[0]), PK(bb[1]), "prm"], writes=[yk])
                    P.op("dve", E.scalar_tensor_tensor(
                        out=y[:, 1:1024], in0=pv[:, 0:1023], scalar=w0, in1=y[:, 1:1024], op0=ALU.mult, op1=ALU.add),
                        reads=[PK(bb[0]), PK(bb[1]), yk, "prm"], writes=[yk])
                    P.op("dve", E.scalar_tensor_tensor(
                        out=y[:, 0:1023], in0=pv[:, 1:1024], scalar=w2, in1=y[:, 0:1023], op0=ALU.mult, op1=ALU.add),
                        reads=[PK(bb[0]), PK(bb[1]), yk, "prm"], writes=[yk])
                    if hf == 0:
                        P.op("act", E.activation(out=ubound[:, ch:ch + 1], in_=pv[:, 1023:1024], func=AF.Copy),
                             reads=[PK(bb[1])], writes=[("ub", ch)])
                        P.op("dve", E.scalar_tensor_tensor(
                            out=y[:, 1023:1024], in0=ps[:, bh, 0:1], scalar=w2, in1=y[:, 1023:1024], op0=ALU.mult, op1=ALU.add),
                            reads=[PK(bh), yk, "prm"], writes=[yk])
                    else:
                        P.op("dve", E.scalar_tensor_tensor(
                            out=y[:, 0:1], in0=ubound[:, ch:ch + 1], scalar=w0, in1=y[:, 0:1], op0=ALU.mult, op1=ALU.add),
                            reads=[("ub", ch), yk, "prm"], writes=[yk])
                    ys.append((y, yk))
                if dbg == "yg" and i == 0 and hf == 0:
                    dump("yg", [(ys[0][0], 1024, [ys[0][1]], True), (ys[1][0], 1024, [ys[1][1]], True)])
                    break
                ge = geb[i % 2]
                gk_ = ("ge", i % 2)
                P.op("act", E.activation(out=ge, in_=ys[0][0], func=AF.Gelu_apprx_tanh),
                     reads=[ys[0][1]], writes=[gk_])
                P.op("dve", E.tensor_tensor(out=act[:, i, :], in0=ge, in1=ys[1][0], op=ALU.mult),
                     reads=[gk_, ys[1][1]], writes=[("act", i)])
            if dbg == "yg":
                break
            if hf == 0 and dump("act", [(act[:, i, :], 1024, [("act", i)], False) for i in range(16)]):
                break
            bar()
            YALL = [("y", w_, i_) for w_ in range(2) for i_ in range(2)] + [("ge", 0), ("ge", 1)]
            for j in range(8):
                wa, ka = stream.use(l, "dn%da" % j)
                wa = wa.rearrange("p (k c) -> p k c", k=11)
                wb, kb = stream.use(l, "dn%db" % j)
                wb = wb.rearrange("p (k c) -> p k c", k=11)
                for tt_ in range(2):
                    b = banks(1)[0]
                    for i in range(22):
                        wsrc, wk = (wa, ka) if i < 11 else (wb, kb)
                        P.op("pe", E.matmul(
                            ps[:, b, :], wsrc[:, i % 11, :], act[:, i, tt_ * 512:(tt_ + 1) * 512], start=(i == 0), stop=(i == 21)),
                            reads=[wk, ("act", i)], writes=[PK(b)])
                    P.op("act", E.activation(out=ffo[:, j, tt_ * 512:(tt_ + 1) * 512], in_=ps[:, b, :], func=AF.Copy),
                         reads=[PK(b)] + (YALL if j == 0 else []), writes=[("ffo", j, tt_)])
            post_norm_add(l, "nfpost", lambda c, i: ffo[:, c, i * 256:(i + 1) * 256], lambda c, i: ("ffo", c, i // 2), 256, 4,
                          None, fsq, ftmp, "f",
                          lambda c, i, hf=hf: (xT[:, c, hf * 1024 + i * 256: hf * 1024 + (i + 1) * 256], [("x", c, hf * 2 + i // 2)]))
        if dbg in ("act", "yg") and dumped[0]:
            break
        bar()
        if dump("x2", [(xT[:, c, :], S, [("x", c, t) for t in range(4)], True) for c in range(8)]):
            break

        ACT_ALL = [("act", i) for i in range(22)]
        for c in range(2):
            P.dma("pool", E.dma_start(out=pTs[:, c, :], in_=pT_d[l, c]),
                  reads=ACT_ALL, writes=[("pT", c)])
        o_, w_ = offs["ple"]
        P.dma("pool", E.dma_start(out=wple.rearrange("p a b -> p (a b)"), in_=ws_d[:, l * LW + o_: l * LW + o_ + w_]),
              reads=ACT_ALL, writes=["wple"])
        for c in range(8):
            for t in range(TT):
                P.op("act", E.activation(out=hT[:, c, t * 512:(t + 1) * 512], in_=xT[:, c, t * 512:(t + 1) * 512], func=AF.Copy),
                     reads=[("x", c, t)], writes=[hk(c, t)])
        for i in range(4):
            wp, kp = stream.use(l, "pg%d" % i)
            wp = wp.rearrange("p (u k c) -> p u k c", u=2, k=8)
            for u_ in range(2):
                j = 2 * i + u_
                for t in range(TT):
                    sl = slice(t * 512, (t + 1) * 512)
                    bG = banks(1)[0]
                    proj_fm(bG, wp[:, u_], kp, lambda kc, t=t: hT[:, kc, t * 512:(t + 1) * 512], lambda kc, t=t: [hk(kc, t)])
                    bP = banks(1)[0]
                    for kc in range(2):
                        P.op("pe", E.matmul(ps[:, bP, :], wple[:, kc, j * 128:(j + 1) * 128], pTs[:, kc, sl],
                                                                                 start=(kc == 0), stop=(kc == 1)),
                             reads=["wple", ("pT", kc)], writes=[PK(bP)])
                    sg_ = ptmp[t % 2]
                    sk_ = ("psg", t % 2)
                    P.op("act", E.activation(out=sg_, in_=ps[:, bG, :], func=AF.Sigmoid),
                         reads=[PK(bG)], writes=[sk_])
                    P.op("dve", E.tensor_tensor(out=sg_, in0=ps[:, bP, :], in1=sg_, op=ALU.mult),
                         reads=[PK(bP), sk_], writes=[sk_])
                    P.op("dve", E.tensor_tensor(out=xT[:, j, sl], in0=xT[:, j, sl], in1=sg_, op=ALU.add),
                         reads=[sk_, ("x", j, t)], writes=[("x", j, t)])
        bar()
        if dump("x3", [(xT[:, c, :], S, [("x", c, t) for t in range(4)], True) for c in range(8)]):
            break

    for c in range(8):
        P.dma("sp", E.dma_start(out=yT_d[c], in_=xT[:, c, :]),
              reads=[("x", c, t) for t in range(4)], writes=[("y", c)], final=True)
    P.finish()
    return nc, P


_CACHE = {}


def prepare_inputs(inputs, nl=L):
    x = np.asarray(inputs["x"], np.float32)
    p = np.asarray(inputs["p"], np.float32)
    ws, offs, LW = pack_weights(inputs, nl)
    prm, lay, n = pack_params(inputs)
    cm, rope = const_tables()
    in_maps = []
    for b in range(NCORES):
        xT = np.ascontiguousarray(x[b].T).reshape(8, 128, S)
        pT = np.ascontiguousarray(p[:, b].transpose(0, 2, 1)).reshape(L, 2, 128, S)
        in_maps.append({"xT": xT, "pT": pT, "ws": ws, "prm": prm, "cm": cm, "rope": rope})
    return in_maps


def kernel(**inputs):
    in_maps = prepare_inputs(inputs)
    if "nc" not in _CACHE:
        _CACHE["nc"] = build(L)[0]
    nc = _CACHE["nc"]
    res = run_bass_kernel_spmd(nc, in_maps, core_ids=list(range(NCORES)))
    out = np.empty((NCORES, S, D), np.float32)
    for b in range(NCORES):
        yT = np.asarray(res.results[b]["yT"]).reshape(D, S)
        out[b] = yT.T
    return out
```

```python
from contextlib import ExitStack
import math
import numpy as np
import concourse.bass as bass
import concourse.mybir as mybir
from concourse.bass_utils import run_bass_kernel_spmd

F32 = mybir.dt.float32
BF16 = mybir.dt.bfloat16
AF = mybir.ActivationFunctionType
ALU = mybir.AluOpType
AX = mybir.AxisListType

D = 1024
S = 2048
L = 4
NCORES = 8
PLE = 256
CONV_CH = 256
CONV_W = 31
FFN = 2816
EPS = 1e-6
TT = 4
NKC = 8
SLOT = 2048
NSLOT = 4
PREFETCH = 2

ENGS = ("pe", "act", "dve", "pool", "sp")


class _Rec:
    def __getattr__(self, name):
        def f(*a, **k):
            return lambda e: getattr(e, name)(*a, **k)
        return f


E = _Rec()
SAME_ENG_DIST = 3


class Op:
    __slots__ = ("eng", "fn", "reads", "writes", "dma", "slot", "idx", "eidx",
                 "waits", "signal", "sigval", "sem")

    def __init__(self, eng, fn, reads, writes, dma=False, slot=None):
        self.eng = eng
        self.fn = fn
        self.reads = tuple(reads)
        self.writes = tuple(writes)
        self.dma = dma
        self.slot = slot
        self.waits = {}
        self.signal = False
        self.sigval = 0
        self.sem = None


class Prog:
    def __init__(self, nc):
        self.nc = nc
        self.ops = []
        self.stack = ExitStack()
        self.finals = []
        self.nbar = 0
        self.bar_extra = {}

    def sb(self, name, shape, dtype):
        return self.stack.enter_context(self.nc.sbuf_tensor(name, list(shape), dtype))

    def ps(self, name, shape, dtype=F32):
        return self.stack.enter_context(self.nc.psum_tensor(name, list(shape), dtype))

    def op(self, eng, fn, reads=(), writes=()):
        psr = [r for r in reads if isinstance(r, tuple) and r and r[0] == "ps"]
        if psr:
            reads = [r for r in reads if not (isinstance(r, tuple) and r and r[0] == "ps")]
            writes = list(writes) + psr
        o = Op(eng, fn, reads, writes)
        self.ops.append(o)
        return o

    def dma(self, eng, fn, reads=(), writes=(), slot=None, final=False):
        if slot is None:
            slot = writes[0]
        o = Op(eng, fn, reads, writes, dma=True, slot=("dma", slot))
        o.signal = True
        if final:
            self.finals.append(("dma", slot))
        self.ops.append(o)
        return o

    def barrier(self, markers):
        n = self.nbar
        self.nbar += 1
        for e, (fn, ex) in markers.items():
            self.op(e, fn, writes=[("bar", n, e)] + list(ex))
        for e, (fn, ex) in markers.items():
            if e == "pe":
                continue
            self.op(e, fn, reads=[("bar", n, e2) for e2 in markers if e2 != e],
                    writes=[("bar2", n, e)] + list(ex))

    def finish(self):
        nc = self.nc
        ops = self.ops
        last_w = {}
        readers = {}
        ecount = {e: 0 for e in ENGS}
        for i, o in enumerate(ops):
            o.idx = i
            o.eidx = ecount[o.eng]
            ecount[o.eng] += 1
        need = [[] for _ in ops]
        for o in ops:
            ds = set()
            for r in o.reads:
                w = last_w.get(r)
                if w is not None:
                    ds.add(w)
            for r in o.writes:
                w = last_w.get(r)
                if w is not None:
                    ds.add(w)
                for rd in readers.get(r, ()):
                    ds.add(rd)
            for r in o.reads:
                readers.setdefault(r, []).append(o.idx)
            for r in o.writes:
                last_w[r] = o.idx
                readers[r] = []
            ds.discard(o.idx)
            for d in sorted(ds):
                a = ops[d]
                if a.dma:
                    need[o.idx].append(d)
                elif a.eng == o.eng:
                    if a.eng in ("pe", "sp"):
                        continue
                    if o.eidx - a.eidx < SAME_ENG_DIST:
                        need[o.idx].append(d)
                else:
                    need[o.idx].append(d)
        for o in ops:
            for d in need[o.idx]:
                ops[d].signal = True
        semcount = {}
        for o in ops:
            if not o.signal:
                continue
            key = o.slot if o.dma else ("eng", o.eng)
            inc = 16 if o.dma else 1
            semcount[key] = semcount.get(key, 0) + inc
            o.sem = key
            o.sigval = semcount[key]
        waited = {e: {} for e in ENGS}
        for o in ops:
            w = {}
            for d in need[o.idx]:
                a = ops[d]
                if waited[o.eng].get(a.sem, 0) >= a.sigval:
                    continue
                if w.get(a.sem, 0) < a.sigval:
                    w[a.sem] = a.sigval
            for k, v in w.items():
                waited[o.eng][k] = v
            o.waits = w
        semkeys = sorted(semcount.keys(), key=str)
        sems = {}
        for i, k in enumerate(semkeys):
            sems[k] = self.stack.enter_context(nc.semaphore("sem%d" % i))
        engobj = {"pe": "tensor", "act": "scalar", "dve": "vector", "pool": "gpsimd", "sp": "sync"}
        by_eng = {e: [o for o in ops if o.eng == e] for e in ENGS}
        final = set(self.finals)
        self.n_sems = len(semkeys)

        def make(ename):
            def body(eng):
                for o in by_eng[ename]:
                    for k, v in o.waits.items():
                        eng.wait_ge(sems[k], v)
                    ins = o.fn(eng)
                    if o.signal:
                        ins.then_inc(sems[o.sem], 16 if o.dma else 1)
                if ename == "sp":
                    for k in semkeys:
                        if k in final:
                            eng.wait_ge(sems[k], semcount[k])
            return body

        with nc.Block() as block:
            for ename in ENGS:
                if not by_eng[ename] and ename != "sp":
                    continue
                getattr(block, engobj[ename])(make(ename))
        self.stack.close()
        return nc


def _perm_half(d):
    return np.concatenate([np.arange(0, d, 2), np.arange(1, d, 2)])


O_AVAL, O_AGATE = 0, 256
O_DQ, O_DK, O_DV = 512, 768, 1024
O_GQ, O_GK, O_GV = 1280, 1792, 1920
O_GATES = 2048


def _win_units():
    u = {}
    u["ag0"] = O_AGATE + np.arange(128)
    u["ag1"] = O_AGATE + 128 + np.arange(128)
    u["av0"] = O_AVAL + np.arange(128)
    u["av1"] = O_AVAL + 128 + np.arange(128)
    p32 = _perm_half(32)
    p64 = _perm_half(64)
    for c in range(2):
        cols = -np.ones(128, np.int64)
        for hh in range(2):
            for s_ in range(2):
                cols[hh * 64 + s_ * 32: hh * 64 + s_ * 32 + 32] = O_DQ + ((2 * c + hh) * 2 + s_) * 32 + p32
        u["dq%d" % c] = cols
        for s_ in range(2):
            cols = -np.ones(128, np.int64)
            for hh in range(2):
                cols[hh * 64 + s_ * 32: hh * 64 + s_ * 32 + 32] = O_DK + ((2 * c + hh) * 2 + s_) * 32 + p32
            u["dk%d" % (2 * s_ + c)] = cols
    for c in range(4):
        cols = np.concatenate([O_GQ + (2 * c) * 64 + p64, O_GQ + (2 * c + 1) * 64 + p64])
        u["gq%d" % c] = cols
    for g in range(2):
        cols = np.concatenate([O_GK + g * 64 + p64, O_GK + g * 64 + p64])
        u["gk%d" % g] = cols
    return u


def _unit_fm(W, cols):
    K = W.shape[0]
    kc = K // 128
    Wc = np.zeros((K, len(cols)), np.float32)
    valid = cols >= 0
    Wc[:, valid] = W[:, cols[valid]]
    return np.ascontiguousarray(Wc.reshape(kc, 128, len(cols)).transpose(1, 0, 2)).reshape(128, -1)


def layer_tiles():
    t = []
    for nm in ("ag", "av", "dqa", "dqb", "dkb", "vd", "gqa", "gqb", "gk", "vg"):
        t.append((nm, 1024 if nm == "vg" else 2048))
    for j in range(8):
        t.append(("mgA%d" % j, 2048))
        t.append(("mgB%d" % j, 2048))
    for i in range(4):
        t.append(("wo%d" % i, 2048))
    for i in range(22):
        t.append(("up%d" % i, 2048))
    for j in range(8):
        t.append(("dn%da" % j, 1408))
        t.append(("dn%db" % j, 1408))
    t.append(("ple", 2048))
    for i in range(4):
        t.append(("pg%d" % i, 2048))
    return t


def tile_offsets():
    offs = {}
    o = 0
    for nm, w in layer_tiles():
        offs[nm] = (o, w)
        o += w
    return offs, o


def pack_weights(inp, nl=L):
    offs, LW = tile_offsets()
    out = np.zeros((128, nl * LW), np.float32)
    wu = _win_units()
    for l in range(nl):
        w_in = np.asarray(inp["w_in"][l])
        base = l * LW

        def put(nm, arr):
            o, w = offs[nm]
            assert arr.shape == (128, w), (nm, arr.shape, w)
            out[:, base + o: base + o + w] = arr

        def two(a, b):
            return np.concatenate([_unit_fm(w_in, wu[a]), _unit_fm(w_in, wu[b])], axis=1)

        put("ag", two("ag0", "ag1"))
        put("av", two("av0", "av1"))
        put("dqa", two("dq0", "dq1"))
        put("dqb", two("dk0", "dk1"))
        put("dkb", two("dk2", "dk3"))
        put("vd", _unit_fm(w_in, O_DV + np.arange(256)))
        put("gqa", two("gq0", "gq1"))
        put("gqb", two("gq2", "gq3"))
        put("gk", two("gk0", "gk1"))
        put("vg", _unit_fm(w_in, O_GV + np.arange(128)))
        wco = np.asarray(inp["w_conv_out"][l])
        wdo = np.asarray(inp["w_diff_out"][l])
        wgo = np.asarray(inp["w_gqa_out"][l])
        for j in range(8):
            cj = j * 128 + np.arange(128)
            gA = _unit_fm(w_in, O_GATES + cj)
            gB = _unit_fm(w_in, O_GATES + 1024 + cj)
            gC = _unit_fm(w_in, O_GATES + 2048 + cj)
            ow = np.concatenate([_unit_fm(wco, cj), _unit_fm(wdo, cj), _unit_fm(wgo, cj)], axis=1)
            put("mgA%d" % j, np.concatenate([gA, gB], axis=1))
            put("mgB%d" % j, np.concatenate([gC, ow], axis=1))
        w_out = np.asarray(inp["w_out"][l])
        for i in range(4):
            put("wo%d" % i, np.concatenate([_unit_fm(w_out, (2 * i) * 128 + np.arange(128)),
                                            _unit_fm(w_out, (2 * i + 1) * 128 + np.arange(128))], axis=1))
        w_up = np.asarray(inp["w_up"][l])
        for i in range(22):
            put("up%d" % i, np.concatenate([_unit_fm(w_up, i * 128 + np.arange(128)),
                                            _unit_fm(w_up, FFN + i * 128 + np.arange(128))], axis=1))
        w_dn = np.asarray(inp["w_down"][l])
        for j in range(8):
            full = _unit_fm(w_dn, j * 128 + np.arange(128)).reshape(128, 22, 128)
            put("dn%da" % j, full[:, :11].reshape(128, -1))
            put("dn%db" % j, full[:, 11:].reshape(128, -1))
        put("ple", _unit_fm(np.asarray(inp["w_ple"][l]), np.arange(1024)))
        wpg = np.asarray(inp["w_ple_gate"][l])
        for i in range(4):
            put("pg%d" % i, np.concatenate([_unit_fm(wpg, (2 * i) * 128 + np.arange(128)),
                                            _unit_fm(wpg, (2 * i + 1) * 128 + np.arange(128))], axis=1))
    return out, offs, LW


def prm_layout():
    o = {}
    n = 0

    def add(nm, w):
        nonlocal n
        o[nm] = (n, w)
        n += w
    add("nmp", 8)
    add("nmpost", 8)
    add("nfp", 8)
    add("nfpost", 8)
    add("cw", 62)
    add("cb", 2)
    add("clg", 2)
    add("clb", 2)
    add("subg", 1)
    add("gqn", 1)
    add("gkn", 1)
    add("fw", 132)
    add("fb", 44)
    add("dl", 128)
    return o, n


def pack_params(inp):
    lay, n = prm_layout()
    out = np.zeros((128, L * n), np.float32)
    p64 = _perm_half(64)
    for l in range(L):
        b = l * n

        def put(nm, arr):
            o, w = lay[nm]
            out[:, b + o: b + o + w] = np.asarray(arr, np.float32).reshape(128, w)

        def fm(v):
            v = np.asarray(v)
            return v.reshape(-1, 128).T
        put("nmp", fm(inp["norm_mix_pre"][l]))
        put("nmpost", fm(inp["norm_mix_post"][l]))
        put("nfp", fm(inp["norm_ffn_pre"][l]))
        put("nfpost", fm(inp["norm_ffn_post"][l]))
        cw = np.asarray(inp["conv_dw_w"][l])
        put("cw", np.concatenate([cw[:, 0:128].T, cw[:, 128:256].T], axis=1))
        put("cb", fm(inp["conv_dw_b"][l]))
        put("clg", fm(inp["conv_ln_g"][l]))
        put("clb", fm(inp["conv_ln_b"][l]))
        put("subg", np.tile(np.asarray(inp["diff_subln_g"][l]), 2).reshape(128, 1))
        put("gqn", np.tile(np.asarray(inp["gqa_q_norm"][l])[p64], 2).reshape(128, 1))
        put("gkn", np.tile(np.asarray(inp["gqa_k_norm"][l])[p64], 2).reshape(128, 1))
        fw = np.asarray(inp["ffn_dw_w"][l])
        put("fw", np.concatenate([fw[k].reshape(44, 128).T for k in range(3)], axis=1))
        put("fb", fm(inp["ffn_dw_b"][l]))
        put("dl", np.tile(np.asarray(inp["diff_lambda"][l]).reshape(1, 128), (128, 1)))
    return out, lay, n


def const_tables():
    ones = np.ones((128, 128), np.float32)
    ident = np.eye(128, dtype=np.float32)
    blk64 = np.zeros((128, 128), np.float32)
    blk64[0:64, 0:64] = 1
    blk64[64:128, 64:128] = 1
    r32 = np.zeros((128, 128), np.float32)
    r64 = np.zeros((128, 128), np.float32)
    for m in range(128):
        r32[(m // 32) * 32 + ((m % 32) + 16) % 32, m] = 1
        r64[(m // 64) * 64 + ((m % 64) + 32) % 64, m] = 1
    cm = np.concatenate([ones, ident, blk64, r32, r64], axis=1)
    t = np.arange(S, dtype=np.float32)
    f32_ = (1.0 / (10000.0 ** (np.arange(0, 32, 2, dtype=np.float32) / 32))).astype(np.float32)
    p = np.arange(128)
    ang1 = t[None, :] * f32_[p % 16][:, None]
    cos1 = np.cos(ang1)
    sin1 = np.sin(ang1) * np.where((p % 32) < 16, -1.0, 1.0)[:, None]
    row = np.floor(t / 64.0)
    col = t - 64.0 * row
    i = p % 32
    pos = np.where((i < 16)[:, None], row[None, :], col[None, :])
    ang2 = pos * f32_[i % 16][:, None]
    cos2 = np.cos(ang2)
    sin2 = np.sin(ang2) * np.where((p % 64) < 32, -1.0, 1.0)[:, None]
    rope = np.concatenate([cos1, sin1, cos2, sin2], axis=1).astype(np.float32)
    return cm.astype(np.float32), rope


def build(nl=L, dbg=None):
    nc = bass.Bass("TRN2", target_bir_lowering=False)
    P = Prog(nc)
    offs, LW = tile_offsets()
    play, NPRM = prm_layout()

    xT_d = nc.dram_tensor("xT", [8, 128, S], F32, kind="ExternalInput").ap()
    pT_d = nc.dram_tensor("pT", [L, 2, 128, S], F32, kind="ExternalInput").ap()
    ws_d = nc.dram_tensor("ws", [128, nl * LW], F32, kind="ExternalInput").ap()
    prm_d = nc.dram_tensor("prm", [128, L * NPRM], F32, kind="ExternalInput").ap()
    cm_d = nc.dram_tensor("cm", [128, 640], F32, kind="ExternalInput").ap()
    rope_d = nc.dram_tensor("rope", [128, 4 * S], F32, kind="ExternalInput").ap()
    yT_d = nc.dram_tensor("yT", [8, 128, S], F32, kind="ExternalOutput").ap()
    dbg_d = None
    if dbg is not None:
        dbg_d = nc.dram_tensor("dbg", [128, 8 * S], F32, kind="ExternalOutput").ap()

    xT = P.sb("xT_sb", [128, 8, S], F32)
    hT = P.sb("hT_sb", [128, 8, S], BF16)
    prm = P.sb("prm_sb", [128, L * NPRM], F32)
    cm = P.sb("cm_sb", [128, 640], BF16)
    lamt = P.sb("lam_sb", [128, L * 4], F32)
    ring = P.sb("ring_sb", [128, NSLOT, SLOT], BF16)
    BIGB = 84 * 1024
    big = P.sb("big_sb", [128, BIGB // 2], BF16)
    ps = P.ps("psum", [128, 8, 512], F32)

    ones_m = cm[:, 0:128]
    ident_m = cm[:, 128:256]
    blk64_m = cm[:, 256:384]
    r32_m = cm[:, 384:512]
    r64_m = cm[:, 512:640]

    def carve(off_b, shape, dtype):
        n = int(np.prod(shape))
        if dtype == BF16:
            assert off_b % 2 == 0
            a = big[:, off_b // 2: off_b // 2 + n]
        else:
            assert off_b % 4 == 0
            a = big[:, off_b // 2: off_b // 2 + 2 * n].bitcast(F32)
        if len(shape) == 1:
            return a
        if len(shape) == 2:
            return a.rearrange("p (a b) -> p a b", a=shape[0])
        if len(shape) == 3:
            return a.rearrange("p (a b c) -> p a b c", a=shape[0], b=shape[1])
        raise ValueError

    KB = 1024
    uA = carve(0, [2, S], BF16)
    oB = carve(8 * KB, [2, S], BF16)
    upad = carve(16 * KB, [2, S + 32], BF16)
    dg = carve(25 * KB, [2, CONV_W, 128], BF16)
    tab = carve(16 * KB, [2, S], BF16)
    qd = carve(24 * KB, [2, S], BF16)
    kd = carve(32 * KB, [4, S], BF16)
    vd = carve(48 * KB, [16, 384], BF16)
    gq = carve(24 * KB, [4, S], BF16)
    gk = carve(40 * KB, [2, S], BF16)
    vg = carve(48 * KB, [16, 384], BF16)
    pexp = carve(60 * KB, [2, 1024], BF16)
    tmpf = [carve((64 + 2 * i) * KB, [512], F32) for i in range(8)]
    tmpb = [carve((64 + 2 * i) * KB, [1024], BF16) for i in range(8)]
    merged = carve(40 * KB, [8, S], BF16)
    dtmp = [carve((72 + 2 * i) * KB, [512], F32) for i in range(3)]
    dsg = [carve((78 + i) * KB, [512], BF16) for i in range(2)]
    mixt = carve(0, [8, 512], F32)
    etmp = [carve((16 + 2 * i) * KB, [512], F32) for i in range(4)]
    esq = [carve((24 + i) * KB, [512], BF16) for i in range(4)]
    act = carve(0, [22, 1024], BF16)
    yg = [carve((44 + 8 * i) * KB, [1024], F32) for i in range(2)]
    yv = [carve((44 + 8 * i + 4) * KB, [1024], F32) for i in range(2)]
    geb = [carve((60 + 2 * i) * KB, [1024], BF16) for i in range(2)]
    halo = carve(64 * KB, [4, 4], F32)
    ffo = carve(44 * KB, [8, 1024], F32)
    fsq = [carve(76 * KB + 512 * i, [256], BF16) for i in range(4)]
    ftmp = [carve((78 + i) * KB, [256], F32) for i in range(4)]
    pTs = carve(0, [2, S], BF16)
    wple = carve(8 * KB, [2, 1024], BF16)
    ptmp = [carve((16 + 2 * i) * KB, [512], F32) for i in range(4)]

    def hk(kc, t):
        return ("h", kc, t)

    H_ALL = [hk(kc, t) for kc in range(8) for t in range(4)]

    bank_ptr = [0]

    def banks(n=1):
        b = bank_ptr[0]
        if b % n:
            b += n - (b % n)
        if b + n > 8:
            b = 0
        bank_ptr[0] = (b + n) % 8
        return list(range(b, b + n))

    def PK(b):
        return ("ps", b)

    attn_ctr = [0, 0]

    def acc_banks():
        i = attn_ctr[0]
        attn_ctr[0] += 1
        return [4, 5] if i % 2 == 0 else [6, 7]

    def sc_banks():
        i = attn_ctr[1]
        attn_ctr[1] += 1
        return [0, 1] if i % 2 == 0 else [2, 3]

    scr = P.sb("scr_sb", [128, 8], F32)

    def bar():
        b = banks(1)[0]
        P.barrier({
            "pe": (E.matmul(ps[0:1, b, 0:1], ones_m[:, 0:1], ones_m[:, 0:1], start=True, stop=True), [PK(b), "cmr"]),
            "act": (E.activation(out=scr[:, 0:1], in_=scr[:, 1:2], func=AF.Copy), []),
            "dve": (E.tensor_copy(out=scr[:, 2:3], in_=scr[:, 3:4]), []),
            "pool": (E.memset(scr[:, 4:5], 0.0), []),
        })

    class Stream:
        def __init__(self):
            self.seq = []
            self.issued = 0
            self.cur = -1

        def plan(self, seq):
            self.seq = seq

        def _issue(self, n):
            l, nm = self.seq[n]
            o, w = offs[nm]
            slot = n % NSLOT
            src = ws_d[:, l * LW + o: l * LW + o + w]
            P.dma("pool", E.dma_start(out=ring[:, slot, 0:w], in_=src),
                  writes=[("ring", slot)])

        def use(self, l, nm):
            self.cur += 1
            n = self.cur
            assert self.seq[n] == (l, nm), (self.seq[n], l, nm)
            while self.issued < min(n + 1 + PREFETCH, len(self.seq)):
                self._issue(self.issued)
                self.issued += 1
            slot = n % NSLOT
            w = offs[nm][1]
            return ring[:, slot, 0:w], ("ring", slot)

    stream = Stream()
    seq = []
    for l in range(nl):
        for nm in ("ag", "av", "dqa", "dqb", "dkb", "vd", "gqa", "gqb", "gk", "vg"):
            seq.append((l, nm))
        for j in range(8):
            seq.append((l, "mgA%d" % j))
            seq.append((l, "mgB%d" % j))
        for t in range(TT):
            for i in range(4):
                seq.append((l, "wo%d" % i))
        for hf in range(2):
            for i in range(22):
                seq.append((l, "up%d" % i))
            for j in range(8):
                seq.append((l, "dn%da" % j))
                seq.append((l, "dn%db" % j))
        for i in range(4):
            seq.append((l, "pg%d" % i))
    stream.plan(seq)

    def proj_fm(bank, wap, wkey, rhs_fn, rkeys_fn, nk=8):
        for kc in range(nk):
            P.op("pe", E.matmul(ps[:, bank, :], wap[:, kc, :], rhs_fn(kc),
                                                 start=(kc == 0), stop=(kc == nk - 1)),
                 reads=[wkey] + rkeys_fn(kc), writes=[PK(bank)])

    def rstd_from_bank(bank, dst, dkey, scale, bias, width=512):
        P.op("act", E.activation(out=dst, in_=ps[:, bank, 0:width], func=AF.Ln, scale=scale, bias=bias),
             reads=[PK(bank)], writes=[dkey])
        P.op("act", E.activation(out=dst, in_=dst, func=AF.Exp, scale=-0.5),
             reads=[dkey], writes=[dkey])

    def pcol(l, nm, i=0):
        o, w = play[nm]
        return prm[:, l * NPRM + o + i: l * NPRM + o + i + 1]

    def norm_to_h(l, gname, tmp_pool, sq_pool, tkey):
        for t in range(TT):
            sl = slice(t * 512, (t + 1) * 512)
            b = banks(1)[0]
            for c in range(8):
                sq = sq_pool[c % len(sq_pool)][:, 0:512]
                sk = (tkey, "sq", c % len(sq_pool))
                P.op("act", E.activation(out=sq, in_=xT[:, c, sl], func=AF.Square),
                     reads=[("x", c, t)], writes=[sk])
                P.op("pe", E.matmul(ps[:, b, :], ones_m, sq, start=(c == 0), stop=(c == 7)),
                     reads=[sk, "cm"], writes=[PK(b)])
            rs = tmp_pool[t % 2]
            rk = (tkey, "rs", t % 2)
            rstd_from_bank(b, rs, rk, 1.0 / D, EPS)
            for c in range(8):
                P.op("dve", E.scalar_tensor_tensor(
                    out=hT[:, c, sl], in0=xT[:, c, sl], scalar=pcol(l, gname, c), in1=rs,
                    op0=ALU.mult, op1=ALU.mult),
                    reads=[("x", c, t), rk, "prm"], writes=[hk(c, t)])

    def post_norm_add(l, gname, src_fn, skey_fn, width, ntile, tok0_fn, sq_pool, tmp_pool, tkey, xt_fn):
        for i in range(ntile):
            b = banks(1)[0]
            for c in range(8):
                sq = sq_pool[c % len(sq_pool)]
                sk = (tkey, "sq", c % len(sq_pool))
                P.op("act", E.activation(out=sq[:, 0:width], in_=src_fn(c, i), func=AF.Square),
                     reads=[skey_fn(c, i)], writes=[sk])
                P.op("pe", E.matmul(ps[:, b, 0:width], ones_m, sq[:, 0:width],
                                                          start=(c == 0), stop=(c == 7)),
                     reads=[sk, "cm"], writes=[PK(b)])
            rs = tmp_pool[i % 2]
            rk = (tkey, "rs", i % 2)
            rstd_from_bank(b, rs[:, 0:width], rk, 1.0 / D, EPS, width=width)
            for c in range(8):
                tk = (tkey, "t", c % 2)
                tb = tmp_pool[2 + c % 2]
                xs, xkeys = xt_fn(c, i)
                P.op("dve", E.scalar_tensor_tensor(
                    out=tb[:, 0:width], in0=src_fn(c, i), scalar=pcol(l, gname, c), in1=rs[:, 0:width],
                    op0=ALU.mult, op1=ALU.mult),
                    reads=[skey_fn(c, i), rk, "prm"], writes=[tk])
                P.op("dve", E.tensor_tensor(out=xs, in0=xs, in1=tb[:, 0:width], op=ALU.add),
                     reads=[tk] + xkeys, writes=xkeys)

    P.op("dve", E.memset(scr[:], 0.0), writes=["scr0"])
    P.dma("sp", E.dma_start(out=prm[:], in_=prm_d), writes=["prm"])
    P.dma("pool", E.dma_start(out=cm[:], in_=cm_d), writes=["cm"])
    for t in range(TT):
        for c in range(8):
            P.dma("sp", E.dma_start(out=xT[:, c, t * 512:(t + 1) * 512], in_=xT_d[c][:, t * 512:(t + 1) * 512]),
                  writes=[("x", c, t)], slot=("xload", c, t))
    lam_inits = [0.8 - 0.6 * math.exp(-0.3 * l) for l in range(L)]
    for l in range(nl):
        o, w = play["dl"]
        dl = prm[:, l * NPRM + o: l * NPRM + o + 128]
        pr = tmpf[0]
        P.op("dve", E.tensor_tensor(
            out=pr[:, 0:64].rearrange("p (a b) -> p a b", a=2),
            in0=dl.rearrange("p (a b c) -> p a b c", a=2, b=2)[:, :, 0, :],
            in1=dl.rearrange("p (a b c) -> p a b c", a=2, b=2)[:, :, 1, :], op=ALU.mult),
            reads=["prm"], writes=["lamtmp"])
        P.op("dve", E.tensor_reduce(
            out=pr[:, 64:66], in_=pr[:, 0:64].rearrange("p (a b) -> p a b", a=2), axis=AX.X, op=ALU.add),
            reads=["lamtmp"], writes=["lamtmp2"])
        P.op("act", E.activation(out=pr[:, 66:68], in_=pr[:, 64:66], func=AF.Exp),
             reads=["lamtmp2"], writes=["lamtmp3"])
        P.op("dve", E.scalar_tensor_tensor(
            out=lamt[:, 4 * l: 4 * l + 1], in0=pr[:, 66:67], scalar=lam_inits[l], in1=pr[:, 67:68],
            op0=ALU.add, op1=ALU.subtract), reads=["lamtmp3"], writes=[("lam", l)])
        P.op("dve", E.tensor_scalar(
            out=lamt[:, 4 * l + 1: 4 * l + 2], in0=lamt[:, 4 * l: 4 * l + 1], scalar1=-1.0, scalar2=None,
            op0=ALU.mult), reads=[("lam", l)], writes=[("nlam", l)])

    dumped = [False]
    stored = set()

    def dump(name, items):
        if dbg != name or dumped[0]:
            return False
        col = 0
        stage = tmpf[7]
        for ap, n, keys, is_f32 in items:
            for s0 in range(0, n, 512):
                w = min(512, n - s0)
                if is_f32:
                    src = ap[:, s0:s0 + w]
                    P.dma("sp", E.dma_start(out=dbg_d[:, col:col + w], in_=src),
                          reads=keys, writes=[("dbg", col)], final=True)
                else:
                    P.op("dve", E.tensor_copy(out=stage[:, 0:w], in_=ap[:, s0:s0 + w]),
                         reads=keys, writes=["dbgstage"])
                    P.dma("sp", E.dma_start(out=dbg_d[:, col:col + w], in_=stage[:, 0:w]),
                          reads=["dbgstage"], writes=[("dbg", col)], final=True)
                col += w
        dumped[0] = True
        return True

    P.op("pe", E.matmul(ps[0:1, 0, 0:1], ones_m[:, 0:1], ones_m[:, 0:1], start=True, stop=True),
         reads=["cm"], writes=[PK(0), "cmr"])
    bar()
    for l in range(nl):
        norm_to_h(l, "nmp", tmpf[0:2], tmpb[2:6], "n0")
        bar()
        if dump("h", [(hT[:, c, :], S, [hk(c, t) for t in range(4)], False) for c in range(8)]):
            break

        P.op("pool", E.memset(upad[:, :, 0:15], 0.0), writes=[("upadL",)])
        P.op("pool", E.memset(upad[:, :, 15 + S: 32 + S], 0.0), writes=[("upadR",)])
        for c in range(2):
            for k in range(CONV_W):
                P.op("dve", E.tensor_scalar(
                    out=dg[:, c, k, :], in0=ident_m, scalar1=pcol(l, "cw", c * 31 + k), scalar2=None, op0=ALU.mult),
                    reads=["cm", "prm"], writes=[("dg", c)])
        wag, kag = stream.use(l, "ag")
        wag = wag.rearrange("p (u k c) -> p u k c", u=2, k=8)
        sg = [tmpb[0], tmpb[1]]
        sgt = {}
        for c in range(2):
            for t in range(TT):
                b = banks(1)[0]
                proj_fm(b, wag[:, c], kag, lambda kc, t=t: hT[:, kc, t * 512:(t + 1) * 512], lambda kc, t=t: [hk(kc, t)])
                dst = tmpb[c * 2 + t // 2][:, (t % 2) * 512:(t % 2) * 512 + 512]
                key = ("sg", c, t)
                sgt[(c, t)] = (dst, key)
                P.op("act", E.activation(out=dst, in_=ps[:, b, :], func=AF.Sigmoid),
                     reads=[PK(b)], writes=[key])
        wav, kav = stream.use(l, "av")
        wav = wav.rearrange("p (u k c) -> p u k c", u=2, k=8)
        for c in range(2):
            for t in range(TT):
                b = banks(1)[0]
                proj_fm(b, wav[:, c], kav, lambda kc, t=t: hT[:, kc, t * 512:(t + 1) * 512], lambda kc, t=t: [hk(kc, t)])
                dst, key = sgt[(c, t)]
                P.op("dve", E.tensor_tensor(
                    out=upad[:, c, 15 + t * 512: 15 + (t + 1) * 512], in0=ps[:, b, :], in1=dst, op=ALU.mult),
                    reads=[PK(b), key], writes=[("upad", c, t)])
        yc = [carve((41 + 2 * i) * KB, [512], F32) for i in range(2)]
        ybf = [carve((45 + i) * KB, [512], BF16) for i in range(2)]
        ysq = [carve((47 + i) * KB, [512], BF16) for i in range(2)]
        mu = tmpf[4]
        va = tmpf[5]
        zt = [tmpf[6], tmpf[7]]
        for t in range(TT):
            cb_ = []
            for c in range(2):
                b = banks(1)[0]
                cb_.append(b)
                for k in range(CONV_W):
                    lo = t * 512 + k
                    rk = [("upad", c, tt_) for tt_ in range(max(0, t - 1), min(TT, t + 2))] + [("upadL",), ("upadR",)]
                    P.op("pe", E.matmul(
                        ps[:, b, :], dg[:, c, k, :], upad[:, c, lo: lo + 512], start=(k == 0), stop=(k == CONV_W - 1)),
                        reads=[("dg", c)] + rk, writes=[PK(b)])
                P.op("act", E.activation(out=yc[c], in_=ps[:, b, :], func=AF.Identity,
                                                             bias=pcol(l, "cb", c)),
                     reads=[PK(b), "prm"], writes=[("yc", c)])
                P.op("act", E.activation(out=ybf[c], in_=ps[:, b, :], func=AF.Identity,
                                                             bias=pcol(l, "cb", c)),
                     reads=[PK(b), "prm"], writes=[("ybf", c)])
                P.op("act", E.activation(out=ysq[c], in_=ps[:, b, :], func=AF.Square,
                                                             bias=pcol(l, "cb", c)),
                     reads=[PK(b), "prm"], writes=[("ysq", c)])
            b1 = banks(1)[0]
            b2 = banks(1)[0]
            for c in range(2):
                P.op("pe", E.matmul(ps[:, b1, :], ones_m, ybf[c], start=(c == 0), stop=(c == 1)),
                     reads=[("ybf", c), "cm"], writes=[PK(b1)])
            for c in range(2):
                P.op("pe", E.matmul(ps[:, b2, :], ones_m, ysq[c], start=(c == 0), stop=(c == 1)),
                     reads=[("ysq", c), "cm"], writes=[PK(b2)])
            P.op("dve", E.tensor_scalar(out=mu, in0=ps[:, b1, :], scalar1=1.0 / CONV_CH, scalar2=None, op0=ALU.mult),
                 reads=[PK(b1)], writes=["mu"])
            P.op("dve", E.tensor_tensor(out=va, in0=mu, in1=mu, op=ALU.mult), reads=["mu"], writes=["va"])
            P.op("dve", E.scalar_tensor_tensor(out=va, in0=ps[:, b2, :], scalar=1.0 / CONV_CH, in1=va,
                                                         op0=ALU.mult, op1=ALU.subtract),
                 reads=[PK(b2), "va"], writes=["va"])
            P.op("act", E.activation(out=va, in_=va, func=AF.Ln, bias=EPS), reads=["va"], writes=["va"])
            P.op("act", E.activation(out=va, in_=va, func=AF.Exp, scale=-0.5), reads=["va"], writes=["va"])
            for c in range(2):
                z = zt[c]
                P.op("dve", E.tensor_tensor(out=z, in0=yc[c], in1=mu, op=ALU.subtract),
                     reads=[("yc", c), "mu"], writes=[("z", c)])
                P.op("dve", E.tensor_tensor(out=z, in0=z, in1=va, op=ALU.mult),
                     reads=[("z", c), "va"], writes=[("z", c)])
                P.op("dve", E.tensor_scalar(out=z, in0=z, scalar1=pcol(l, "clg", c),
                                                                scalar2=pcol(l, "clb", c), op0=ALU.mult, op1=ALU.add),
                     reads=[("z", c), "prm"], writes=[("z", c)])
                P.op("act", E.activation(out=uA[:, c, t * 512:(t + 1) * 512], in_=z, func=AF.Silu),
                     reads=[("z", c)], writes=[("uA", c, t)])
        bar()
        if dump("uA", [(uA[:, c, :], S, [("uA", c, t) for t in range(4)], False) for c in range(2)]):
            break

        P.dma("pool", E.dma_start(out=tab[:, 0, :], in_=rope_d[:, 0:S]),
              reads=[("upad", c, t) for c in range(2) for t in range(4)] + [("dg", 0), ("dg", 1)],
              writes=[("tab", 0)])
        P.dma("pool", E.dma_start(out=tab[:, 1, :], in_=rope_d[:, S:2 * S]),
              reads=[("upad", c, t) for c in range(2) for t in range(4)] + [("dg", 0), ("dg", 1)],
              writes=[("tab", 1)])
        P.op("pool", E.memset(vd[:, :, 64:128], 1.0), reads=[("dg", 0), ("dg", 1)], writes=[("vd1",)])
        P.op("pool", E.memset(vd[:, :, 256:320], 1.0), reads=[("dg", 0), ("dg", 1)], writes=[("vd1b",)])

        rope_ctr = [0]

        def lazy_tile(name):
            cache = []

            def get():
                if not cache:
                    w_, k_ = stream.use(l, name)
                    cache.append((w_.rearrange("p (u k c) -> p u k c", u=2, k=8), k_))
                return cache[0]
            return get

        def rope_stages(wget, uidx, dst, dchunk, dname, rmat, norm=None):
            stages = []
            for t in range(TT):
                st = {}

                def s1(t=t, st=st):
                    wslot, wkey = wget()
                    n = rope_ctr[0]
                    rope_ctr[0] += 1
                    par = n % 2
                    st["par"] = par
                    bA = banks(1)[0]
                    st["bA"] = bA
                    proj_fm(bA, wslot[:, uidx], wkey, lambda kc, t=t: hT[:, kc, t * 512:(t + 1) * 512], lambda kc, t=t: [hk(kc, t)])
                    qraw = tmpb[par][:, 0:512]
                    qk = ("qraw", par)
                    if norm is None:
                        P.op("act", E.activation(out=qraw, in_=ps[:, bA, :], func=AF.Copy),
                             reads=[PK(bA)], writes=[qk])
                    else:
                        P.op("act", E.activation(out=qraw, in_=ps[:, bA, :], func=AF.Copy, scale=pcol(l, norm)),
                             reads=[PK(bA), "prm"], writes=[qk])
                        sq = tmpb[par][:, 512:1024]
                        P.op("act", E.activation(out=sq, in_=ps[:, bA, :], func=AF.Square),
                             reads=[PK(bA)], writes=[("qsq", par)])

                def s2(t=t, st=st):
                    par = st["par"]
                    bA = st["bA"]
                    sl = slice(t * 512, (t + 1) * 512)
                    qraw = tmpb[par][:, 0:512]
                    qk = ("qraw", par)
                    sq = tmpb[par][:, 512:1024]
                    sk = ("qsq", par)
                    bB = banks(1)[0]
                    P.op("pe", E.matmul(ps[:, bB, :], rmat, qraw, start=True, stop=True),
                         reads=[qk, "cm"], writes=[PK(bB)])
                    if norm is not None:
                        bC = banks(1)[0]
                        P.op("pe", E.matmul(ps[:, bC, :], blk64_m, sq, start=True, stop=True),
                             reads=[sk, "cm"], writes=[PK(bC)])
                    t1 = tmpf[2 + par]
                    t2 = tmpf[4 + par]
                    k1 = ("t1", par)
                    k2 = ("t2", par)
                    if norm is None:
                        P.op("dve", E.tensor_tensor(out=t1, in0=ps[:, bA, :], in1=tab[:, 0, sl], op=ALU.mult),
                             reads=[PK(bA), ("tab", 0)], writes=[k1])
                    else:
                        P.op("dve", E.scalar_tensor_tensor(
                            out=t1, in0=ps[:, bA, :], scalar=pcol(l, norm), in1=tab[:, 0, sl], op0=ALU.mult, op1=ALU.mult),
                            reads=[PK(bA), ("tab", 0), "prm"], writes=[k1])
                    P.op("dve", E.tensor_tensor(out=t2, in0=ps[:, bB, :], in1=tab[:, 1, sl], op=ALU.mult),
                         reads=[PK(bB), ("tab", 1)], writes=[k2])
                    if norm is None:
                        P.op("dve", E.tensor_tensor(out=dst[:, dchunk, sl], in0=t1, in1=t2, op=ALU.add),
                             reads=[k1, k2], writes=[(dname, dchunk, t)])
                    else:
                        rs = tmpf[6 + par]
                        rk = ("qrs", par)
                        rstd_from_bank(bC, rs, rk, 1.0 / 64, EPS)
                        P.op("dve", E.tensor_tensor(out=t1, in0=t1, in1=t2, op=ALU.add),
                             reads=[k1, k2], writes=[k1])
                        P.op("dve", E.tensor_tensor(out=dst[:, dchunk, sl], in0=t1, in1=rs, op=ALU.mult),
                             reads=[k1, rk], writes=[(dname, dchunk, t)])
                stages.append((s1, s2))
            return stages

        def run_stages(stages):
            stages[0][0]()
            for i_ in range(len(stages)):
                if i_ + 1 < len(stages):
                    stages[i_ + 1][0]()
                stages[i_][1]()

        g_ = lazy_tile("dqa")
        stg = rope_stages(g_, 0, qd, 0, "qd", r32_m) + rope_stages(g_, 1, qd, 1, "qd", r32_m)
        g_ = lazy_tile("dqb")
        stg += rope_stages(g_, 0, kd, 0, "kd", r32_m) + rope_stages(g_, 1, kd, 1, "kd", r32_m)
        g_ = lazy_tile("dkb")
        stg += rope_stages(g_, 0, kd, 2, "kd", r32_m) + rope_stages(g_, 1, kd, 3, "kd", r32_m)
        run_stages(stg)
        wv, kv = stream.use(l, "vd")
        wv = wv.rearrange("p (k c) -> p k c", k=8)
        for kt in range(16):
            b = banks(1)[0]
            for kc in range(8):
                P.op("pe", E.matmul(ps[:, b, 0:256], hT[:, kc, kt * 128:(kt + 1) * 128], wv[:, kc, :],
                                                                 start=(kc == 0), stop=(kc == 7)),
                     reads=[kv, hk(kc, kt // 4)], writes=[PK(b)])
            P.op("act", E.activation(
                out=vd[:, kt, 0:256].rearrange("p (a b) -> p a b", a=2)[:, :, 0:64],
                in_=ps[:, b, 0:128].rearrange("p (a b) -> p a b", a=2), func=AF.Copy),
                reads=[PK(b)], writes=[("vd", kt, 0)])
            P.op("act", E.activation(
                out=vd[:, kt, 192:448 - 64].rearrange("p (a b) -> p a b", b=64)[:, 0:3:2, :],
                in_=ps[:, b, 128:256].rearrange("p (a b) -> p a b", a=2), func=AF.Copy),
                reads=[PK(b)], writes=[("vd", kt, 1)])
        bar()
        if dump("qd", [(qd[:, c, :], S, [("qd", c, t) for t in range(4)], False) for c in range(2)] +
                [(kd[:, c, :], S, [("kd", c, t) for t in range(4)], False) for c in range(4)]):
            break

        pexp_l = [pexp[:, 0, :], pexp[:, 1, :], tmpb[7]]
        step_ctr = [0]

        def run_attn(units):
            steps = [(u, kc) for u in units for kc in range(16)]
            state = {}

            def emit_S(i):
                u, kc = steps[i]
                if kc == 0:
                    u["acc"] = acc_banks()
                sb_ = sc_banks()
                for s_ in range(2):
                    kap, kkeys = u["k_fn"](s_, kc)
                    qap, qkeys = u["q"][s_]
                    P.op("pe", E.matmul(ps[:, sb_[s_], :], kap, qap, start=True, stop=True),
                         reads=kkeys + qkeys, writes=[PK(sb_[s_])])
                state[i] = sb_

            def emit_X(i):
                u, kc = steps[i]
                sb_ = state.pop(i)
                n = step_ctr[0]
                step_ctr[0] += 1
                pe_ = pexp_l[n % 3]
                pk_ = ("pexp", n % 3)
                P.op("act", E.activation(out=pe_.rearrange("p (a b) -> p a b", a=2), in_=ps[:, sb_[0]:sb_[0] + 2, :],
                                         func=AF.Exp, scale=u["scale"]),
                     reads=[PK(sb_[0]), PK(sb_[1])], writes=[pk_])
                state[("x", i)] = (pe_, pk_)

            def emit_V(i):
                u, kc = steps[i]
                pe_, pk_ = state.pop(("x", i))
                acc = u["acc"]
                for s_ in range(2):
                    vap, vkeys = u["v_fn"](s_, kc)
                    P.op("pe", E.matmul(ps[:, acc[s_], :], vap, pe_[:, s_ * 512:(s_ + 1) * 512],
                                        start=(kc == 0), stop=(kc == 15)),
                         reads=vkeys + [pk_], writes=[PK(acc[s_])])
                if kc == 15:
                    cont = u["post"](acc)
                    if cont is not None:
                        pending.append((i + 12, cont))

            pending = []
            n_ = len(steps)
            emit_S(0)
            if n_ > 1:
                emit_S(1)
            for i in range(n_):
                emit_X(i)
                if i + 2 < n_:
                    emit_S(i + 2)
                emit_V(i)
                while pending and pending[0][0] <= i:
                    pending.pop(0)[1]()
            while pending:
                pending.pop(0)[1]()

        o12 = [tmpf[3], tmpf[4]]
        odf = [tmpf[0], tmpf[1]]
        units = []
        for pair in range(2):
            for qt in range(TT):
                for sm in range(2):
                    def k_fn(hh, kc, pair=pair, sm=sm):
                        return kd[64 * hh:64 * hh + 64, 2 * sm + pair, kc * 128:(kc + 1) * 128], [("kd", 2 * sm + pair, kc // 4)]

                    def v_fn(hh, kc, pair=pair):
                        vlo = pair * 192 + (0 if hh == 0 else 64)
                        return vd[:, kc, vlo:vlo + 128], [("vd", kc, pair), ("vd1",), ("vd1b",)]

                    def post(acc, pair=pair, qt=qt, sm=sm):
                        rd = tmpf[2]
                        o_ = o12[sm]
                        P.op("dve", E.reciprocal(out=rd[64:128, :], in_=ps[64:128, acc[0], :]),
                             reads=[PK(acc[0])], writes=[("rd", 0)])
                        P.op("dve", E.reciprocal(out=rd[0:64, :], in_=ps[0:64, acc[1], :]),
                             reads=[PK(acc[1])], writes=[("rd", 1)])
                        P.op("dve", E.tensor_tensor(out=o_[0:64, :], in0=ps[0:64, acc[0], :], in1=rd[64:128, :], op=ALU.mult),
                             reads=[PK(acc[0]), ("rd", 0)], writes=[("o12", sm, 0)])
                        P.op("dve", E.tensor_tensor(out=o_[64:128, :], in0=ps[64:128, acc[1], :], in1=rd[0:64, :], op=ALU.mult),
                             reads=[PK(acc[1]), ("rd", 1)], writes=[("o12", sm, 1)])
                        if sm == 0:
                            return None
                        od = odf[qt % 2]
                        P.op("dve", E.scalar_tensor_tensor(
                            out=od, in0=o12[1], scalar=lamt[:, 4 * l + 1: 4 * l + 2], in1=o12[0],
                            op0=ALU.mult, op1=ALU.add),
                            reads=[("o12", 0, 0), ("o12", 0, 1), ("o12", 1, 0), ("o12", 1, 1), ("nlam", l)],
                            writes=[("od", qt % 2)])
                        osq = tmpb[5][:, 0:512]

                        def cont(acc=acc, od=od, pair=pair, qt=qt, osq=osq):
                            P.op("act", E.activation(out=osq, in_=od, func=AF.Square),
                                 reads=[("od", qt % 2)], writes=["osq"])
                            b = acc[0]
                            P.op("pe", E.matmul(ps[:, b, :], blk64_m, osq, start=True, stop=True),
                                 reads=["osq", "cm"], writes=[PK(b)])
                            rs = tmpf[6]
                            li = lam_inits[l]
                            rstd_from_bank(b, rs, "ors", 1.0 / (64 * (1 - li) ** 2), EPS / (1 - li) ** 2)
                            P.op("dve", E.scalar_tensor_tensor(
                                out=oB[:, pair, qt * 512:(qt + 1) * 512], in0=od, scalar=pcol(l, "subg"), in1=rs,
                                op0=ALU.mult, op1=ALU.mult),
                                reads=[("od", qt % 2), "ors", "prm"], writes=[("oB", pair, qt)])
                        return cont

                    q = [(qd[64 * hh:64 * hh + 64, pair, qt * 512:(qt + 1) * 512], [("qd", pair, qt)]) for hh in range(2)]
                    units.append(dict(q=q, k_fn=k_fn, v_fn=v_fn, scale=32 ** -0.5, post=post))
        run_attn(units)
        bar()
        if dump("oB", [(oB[:, c, :], S, [("oB", c, t) for t in range(4)], False) for c in range(2)]):
            break

        QD_KD = [("qd", c, t) for c in range(2) for t in range(4)] + [("kd", c, t) for c in range(4) for t in range(4)]
        VD_ALL = [("vd", kt, i) for kt in range(16) for i in range(2)] + [("vd1",), ("vd1b",)]
        P.dma("pool", E.dma_start(out=tab[:, 0, :], in_=rope_d[:, 2 * S:3 * S]), writes=[("tab", 0)])
        P.dma("pool", E.dma_start(out=tab[:, 1, :], in_=rope_d[:, 3 * S:4 * S]), writes=[("tab", 1)])
        P.op("pool", E.memset(vg[:, :, 0:64], 1.0), reads=VD_ALL, writes=[("vg1", 0)])
        P.op("pool", E.memset(vg[:, :, 128:256], 1.0), reads=VD_ALL, writes=[("vg1", 1)])
        P.op("pool", E.memset(vg[:, :, 320:384], 1.0), reads=VD_ALL, writes=[("vg1", 2)])
        VG1 = [("vg1", i) for i in range(3)]

        def gq_alias_guard(keys):
            return keys

        g_ = lazy_tile("gqa")
        stg = rope_stages(g_, 0, gq, 0, "gq", r64_m, norm="gqn") + rope_stages(g_, 1, gq, 1, "gq", r64_m, norm="gqn")
        g_ = lazy_tile("gqb")
        stg += rope_stages(g_, 0, gq, 2, "gq", r64_m, norm="gqn") + rope_stages(g_, 1, gq, 3, "gq", r64_m, norm="gqn")
        g_ = lazy_tile("gk")
        stg += rope_stages(g_, 0, gk, 0, "gk", r64_m, norm="gkn") + rope_stages(g_, 1, gk, 1, "gk", r64_m, norm="gkn")
        run_stages(stg)
        wv, kv = stream.use(l, "vg")
        wv = wv.rearrange("p (k c) -> p k c", k=8)
        for kt in range(16):
            b = banks(1)[0]
            for kc in range(8):
                P.op("pe", E.matmul(ps[:, b, 0:128], hT[:, kc, kt * 128:(kt + 1) * 128], wv[:, kc, :],
                                                                 start=(kc == 0), stop=(kc == 7)),
                     reads=[kv, hk(kc, kt // 4)], writes=[PK(b)])
            P.op("act", E.activation(
                out=vg[:, kt, :].rearrange("p (a b) -> p a b", a=2)[:, :, 64:128],
                in_=ps[:, b, 0:128].rearrange("p (a b) -> p a b", a=2), func=AF.Copy),
                reads=[PK(b)] + VD_ALL, writes=[("vg", kt)])
        bar()
        if dump("gq", [(gq[:, c, :], S, [("gq", c, t) for t in range(4)], False) for c in range(4)] +
                [(gk[:, c, :], S, [("gk", c, t) for t in range(4)], False) for c in range(2)]):
            break
        units = []
        for c in range(4):
            g = c // 2
            for qt in range(TT):
                def k_fn(s_, kc, g=g):
                    return gk[64 * s_:64 * s_ + 64, g, kc * 128:(kc + 1) * 128], [("gk", g, kc // 4)]

                def v_fn(s_, kc, g=g):
                    vlo = g * 192 + (64 if s_ == 0 else 0)
                    return vg[:, kc, vlo:vlo + 128], [("vg", kc)] + VG1

                def post(acc, c=c, qt=qt):
                    rd = tmpf[2]
                    P.op("dve", E.reciprocal(out=rd[64:128, :], in_=ps[64:128, acc[0], :]),
                         reads=[PK(acc[0])], writes=[("rdg", 0)])
                    P.op("dve", E.reciprocal(out=rd[0:64, :], in_=ps[0:64, acc[1], :]),
                         reads=[PK(acc[1])], writes=[("rdg", 1)])
                    P.op("dve", E.tensor_tensor(
                        out=gq[0:64, c, qt * 512:(qt + 1) * 512], in0=ps[0:64, acc[0], :], in1=rd[64:128, :], op=ALU.mult),
                        reads=[PK(acc[0]), ("rdg", 0)], writes=[("gq", c, qt)])
                    P.op("dve", E.tensor_tensor(
                        out=gq[64:128, c, qt * 512:(qt + 1) * 512], in0=ps[64:128, acc[1], :], in1=rd[0:64, :], op=ALU.mult),
                        reads=[PK(acc[1]), ("rdg", 1), ("gq", c, qt)], writes=[("gq", c, qt)])

                q = [(gq[64 * s_:64 * s_ + 64, c, qt * 512:(qt + 1) * 512], [("gq", c, qt)]) for s_ in range(2)]
                units.append(dict(q=q, k_fn=k_fn, v_fn=v_fn, scale=64 ** -0.5, post=post))
        run_attn(units)
        oC = gq
        bar()
        if dump("oC", [(oC[:, c, :], S, [("gq", c, t) for t in range(4)], False) for c in range(4)]):
            break

        GKV = [("gk", c, t) for c in range(2) for t in range(4)] + [("vg", kt) for kt in range(16)] + VG1 + \
              [("pexp", 0), ("pexp", 1)]
        first_merge = [True]
        for j in range(8):
            wA, kA = stream.use(l, "mgA%d" % j)
            wA = wA.rearrange("p (u k c) -> p u k c", u=2, k=8)
            wB, kB = stream.use(l, "mgB%d" % j)
            wB = wB.rearrange("p (u k c) -> p u k c", u=2, k=8)
            gw = [(wA[:, 0], kA), (wA[:, 1], kA), (wB[:, 0], kB)]
            ow = wB[:, 1]
            srcs = [(uA, "uA", 2, 0), (oB, "oB", 2, 2), (oC, "gq", 4, 4)]
            for t in range(TT):
                sl = slice(t * 512, (t + 1) * 512)
                m = dtmp[0 + (t % 2)]
                mk_ = ("m", t % 2)
                for x in range(3):
                    bG = banks(1)[0]
                    proj_fm(bG, gw[x][0], gw[x][1], lambda kc, t=t: hT[:, kc, t * 512:(t + 1) * 512], lambda kc, t=t: [hk(kc, t)])
                    src, sname, nk, k0 = srcs[x]
                    bO = banks(1)[0]
                    for kc in range(nk):
                        P.op("pe", E.matmul(
                            ps[:, bO, :], ow[:, k0 + kc, :], src[:, kc, sl], start=(kc == 0), stop=(kc == nk - 1)),
                            reads=[kB, (sname, kc, t)], writes=[PK(bO)])
                    sgx = dsg[x % 2]
                    sgk = ("dsg", x % 2)
                    extra = GKV if first_merge[0] else []
                    P.op("act", E.activation(out=sgx, in_=ps[:, bG, :], func=AF.Sigmoid),
                         reads=[PK(bG)], writes=[sgk])
                    if x == 0:
                        P.op("dve", E.tensor_tensor(out=m, in0=ps[:, bO, :], in1=sgx, op=ALU.mult),
                             reads=[PK(bO), sgk], writes=[mk_])
                    else:
                        tb = dtmp[2]
                        P.op("dve", E.tensor_tensor(out=tb, in0=ps[:, bO, :], in1=sgx, op=ALU.mult),
                             reads=[PK(bO), sgk], writes=["dtb"])
                        if x == 1:
                            P.op("dve", E.tensor_tensor(out=m, in0=m, in1=tb, op=ALU.add),
                                 reads=[mk_, "dtb"], writes=[mk_])
                        else:
                            P.op("dve", E.tensor_tensor(out=merged[:, j, sl], in0=m, in1=tb, op=ALU.add),
                                 reads=[mk_, "dtb"] + extra, writes=[("mg", j, t)])
                            first_merge[0] = False
        bar()
        if dump("merged", [(merged[:, c, :], S, [("mg", c, t) for t in range(4)], False) for c in range(8)]):
            break

        mixbuf = [[mixt[:, jo, :] for jo in range(8)],
                  [carve((28 + 2 * jo) * KB, [512], F32) for jo in range(6)] + [carve((72 + 2 * jo) * KB, [512], F32) for jo in range(2)]]

        def wout_tile(t):
            sl = slice(t * 512, (t + 1) * 512)
            for i4 in range(4):
                wo_, kwo = stream.use(l, "wo%d" % i4)
                wo_ = wo_.rearrange("p (u k c) -> p u k c", u=2, k=8)
                for u_ in range(2):
                    jo = 2 * i4 + u_
                    b = banks(1)[0]
                    for j in range(8):
                        P.op("pe", E.matmul(ps[:, b, :], wo_[:, u_, j, :], merged[:, j, sl],
                                            start=(j == 0), stop=(j == 7)),
                             reads=[kwo, ("mg", j, t)], writes=[PK(b)])
                    P.op("act", E.activation(out=mixbuf[t % 2][jo], in_=ps[:, b, :], func=AF.Copy),
                         reads=[PK(b)], writes=[("mix", t % 2, jo)])

        wout_tile(0)
        for t in range(TT):
            sl = slice(t * 512, (t + 1) * 512)
            if t + 1 < TT:
                wout_tile(t + 1)
            post_norm_add(l, "nmpost", lambda c, i, t=t: mixbuf[t % 2][c], lambda c, i, t=t: ("mix", t % 2, c), 512, 1,
                          None, esq, etmp, "e", lambda c, i, t=t, sl=sl: (xT[:, c, sl], [("x", c, t)]))
        bar()
        if dump("x1", [(xT[:, c, :], S, [("x", c, t) for t in range(4)], True) for c in range(8)]):
            break

        norm_to_h(l, "nfp", etmp[0:2], esq, "n1")
        if dbg == "h2":
            bar()
            dump("h2", [(hT[:, c, :], S, [hk(c, t) for t in range(4)], False) for c in range(8)])
            break
        for hf in range(2):
            tok0 = hf * 1024
            bar()
            for i in range(22):
                wu_, ku = stream.use(l, "up%d" % i)
                wu_ = wu_.rearrange("p (u k c) -> p u k c", u=2, k=8)
                ys = []
                for which in range(2):
                    ch = i + 22 * which
                    bb = banks(2)
                    for tt_ in range(2):
                        t = hf * 2 + tt_
                        proj_fm(bb[tt_], wu_[:, which], ku, lambda kc, t=t: hT[:, kc, t * 512:(t + 1) * 512],
                                lambda kc, t=t: [hk(kc, t)])
                    bh = banks(1)[0]
                    for kc in range(8):
                        P.op("pe", E.matmul(ps[:, bh, 0:2], wu_[:, which, kc, :], hT[:, kc, 1023:1025],
                                                                                  start=(kc == 0), stop=(kc == 7)),
                             reads=[ku, hk(kc, 1), hk(kc, 2)], writes=[PK(bh)])
                    y = (yg if which == 0 else yv)[i % 2]
                    yk = ("y", which, i % 2)
                    w0 = pcol(l, "fw", 0 * 44 + ch)
                    w1 = pcol(l, "fw", 1 * 44 + ch)
                    w2 = pcol(l, "fw", 2 * 44 + ch)
                    bcol = pcol(l, "fb", ch)
                    pv = ps[:, bb[0]:bb[0] + 2, :].rearrange("p a b -> p (a b)")
                    extra = []
                    P.op("act", E.activation(out=y, in_=pv, func=AF.Identity, scale=w1, bias=bcol),
                         reads=[PK(bb[0]), PK(bb[1]), "prm"], writes=[yk])
                    P.op("dve", E.scalar_tensor_tensor(
                        out=y[:, 1:1024], in0=pv[:, 0:1023], scalar=w0, in1=y[:, 1:1024], op0=ALU.mult, op1=ALU.add),
                        reads=[PK(bb[0]), PK(bb[1]), yk, "prm"], writes=[yk])
                    P.op("dve", E.scalar_tensor_tensor(
                        out=y[:, 0:1023], in0=pv[:, 1:1024], scalar=w2, in1=y[:, 0:1023], op0=ALU.mult, op1=ALU.add),
                        reads=[PK(bb[0]), PK(bb[1]), yk, "prm"], writes=[yk])
                    if hf == 0:
                        P.op("dve", E.scalar_tensor_tensor(
                            out=y[:, 1023:1024], in0=ps[:, bh, 1:2], scalar=w2, in1=y[:, 1023:1024], op0=ALU.mult, op1=ALU.add),
                            reads=[PK(bh), yk, "prm"], writes=[yk])
                    else:
                        P.op("dve", E.scalar_tensor_tensor(
                            out=y[:, 0:1], in0=ps[:, bh, 0:1], scalar=w0, in1=y[:, 0:1], op0=ALU.mult, op1=ALU.add),
                            reads=[PK(bh), yk, "prm"], writes=[yk])
                    ys.append((y, yk))
                if dbg == "yg" and i == 0 and hf == 0:
                    dump("yg", [(ys[0][0], 1024, [ys[0][1]], True), (ys[1][0], 1024, [ys[1][1]], True)])
                    break
                ge = geb[i % 2]
                gk_ = ("ge", i % 2)
                P.op("act", E.activation(out=ge, in_=ys[0][0], func=AF.Gelu_apprx_tanh),
                     reads=[ys[0][1]], writes=[gk_])
                P.op("dve", E.tensor_tensor(out=act[:, i, :], in0=ge, in1=ys[1][0], op=ALU.mult),
                     reads=[gk_, ys[1][1]], writes=[("act", i)])
            if dbg == "yg":
                break
            if hf == 0 and dump("act", [(act[:, i, :], 1024, [("act", i)], False) for i in range(16)]):
                break
            bar()
            YALL = [("y", w_, i_) for w_ in range(2) for i_ in range(2)] + [("ge", 0), ("ge", 1)]
            for j in range(8):
                wa, ka = stream.use(l, "dn%da" % j)
                wa = wa.rearrange("p (k c) -> p k c", k=11)
                wb, kb = stream.use(l, "dn%db" % j)
                wb = wb.rearrange("p (k c) -> p k c", k=11)
                for tt_ in range(2):
                    b = banks(1)[0]
                    for i in range(22):
                        wsrc, wk = (wa, ka) if i < 11 else (wb, kb)
                        P.op("pe", E.matmul(
                            ps[:, b, :], wsrc[:, i % 11, :], act[:, i, tt_ * 512:(tt_ + 1) * 512], start=(i == 0), stop=(i == 21)),
                            reads=[wk, ("act", i)], writes=[PK(b)])
                    P.op("act", E.activation(out=ffo[:, j, tt_ * 512:(tt_ + 1) * 512], in_=ps[:, b, :], func=AF.Copy),
                         reads=[PK(b)] + (YALL if j == 0 else []), writes=[("ffo", j, tt_)])
            post_norm_add(l, "nfpost", lambda c, i: ffo[:, c, i * 256:(i + 1) * 256], lambda c, i: ("ffo", c, i // 2), 256, 4,
                          None, fsq, ftmp, "f",
                          lambda c, i, hf=hf: (xT[:, c, hf * 1024 + i * 256: hf * 1024 + (i + 1) * 256], [("x", c, hf * 2 + i // 2)]))
        if dbg in ("act", "yg") and dumped[0]:
            break
        bar()
        if dump("x2", [(xT[:, c, :], S, [("x", c, t) for t in range(4)], True) for c in range(8)]):
            break

        ACT_ALL = [("act", i) for i in range(22)]
        for c in range(2):
            P.dma("pool", E.dma_start(out=pTs[:, c, :], in_=pT_d[l, c]),
                  reads=ACT_ALL, writes=[("pT", c)])
        o_, w_ = offs["ple"]
        P.dma("pool", E.dma_start(out=wple.rearrange("p a b -> p (a b)"), in_=ws_d[:, l * LW + o_: l * LW + o_ + w_]),
              reads=ACT_ALL, writes=["wple"])
        for c in range(8):
            for t in range(TT):
                P.op("act", E.activation(out=hT[:, c, t * 512:(t + 1) * 512], in_=xT[:, c, t * 512:(t + 1) * 512], func=AF.Copy),
                     reads=[("x", c, t)], writes=[hk(c, t)])
        for i in range(4):
            wp, kp = stream.use(l, "pg%d" % i)
            wp = wp.rearrange("p (u k c) -> p u k c", u=2, k=8)
            for u_ in range(2):
                j = 2 * i + u_
                for t in range(TT):
                    sl = slice(t * 512, (t + 1) * 512)
                    bG = banks(1)[0]
                    proj_fm(bG, wp[:, u_], kp, lambda kc, t=t: hT[:, kc, t * 512:(t + 1) * 512], lambda kc, t=t: [hk(kc, t)])
                    bP = banks(1)[0]
                    for kc in range(2):
                        P.op("pe", E.matmul(ps[:, bP, :], wple[:, kc, j * 128:(j + 1) * 128], pTs[:, kc, sl],
                                                                                 start=(kc == 0), stop=(kc == 1)),
                             reads=["wple", ("pT", kc)], writes=[PK(bP)])
                    sg_ = ptmp[t % 2]
                    sk_ = ("psg", t % 2)
                    P.op("act", E.activation(out=sg_, in_=ps[:, bG, :], func=AF.Sigmoid),
                         reads=[PK(bG)], writes=[sk_])
                    P.op("dve", E.tensor_tensor(out=sg_, in0=ps[:, bP, :], in1=sg_, op=ALU.mult),
                         reads=[PK(bP), sk_], writes=[sk_])
                    P.op("dve", E.tensor_tensor(out=xT[:, j, sl], in0=xT[:, j, sl], in1=sg_, op=ALU.add),
                         reads=[sk_, ("x", j, t)], writes=[("x", j, t)])
                if l == nl - 1 and dbg is None:
                    P.dma("sp", E.dma_start(out=yT_d[j], in_=xT[:, j, :]),
                          reads=[("x", j, t_) for t_ in range(4)], writes=[("y", j)], final=True)
                    stored.add(j)
        bar()
        if dump("x3", [(xT[:, c, :], S, [("x", c, t) for t in range(4)], True) for c in range(8)]):
            break

    for c in range(8):
        if c in stored:
            continue
        P.dma("sp", E.dma_start(out=yT_d[c], in_=xT[:, c, :]),
              reads=[("x", c, t) for t in range(4)], writes=[("y", c)], final=True)
    P.finish()
    return nc, P


_CACHE = {}


def prepare_inputs(inputs, nl=L):
    x = np.asarray(inputs["x"], np.float32)
    p = np.asarray(inputs["p"], np.float32)
    ws, offs, LW = pack_weights(inputs, nl)
    prm, lay, n = pack_params(inputs)
    cm, rope = const_tables()
    in_maps = []
    for b in range(NCORES):
        xT = np.ascontiguousarray(x[b].T).reshape(8, 128, S)
        pT = np.ascontiguousarray(p[:, b].transpose(0, 2, 1)).reshape(L, 2, 128, S)
        in_maps.append({"xT": xT, "pT": pT, "ws": ws, "prm": prm, "cm": cm, "rope": rope})
    return in_maps


def kernel(**inputs):
    in_maps = prepare_inputs(inputs)
    if "nc" not in _CACHE:
        _CACHE["nc"] = build(L)[0]
    nc = _CACHE["nc"]
    res = run_bass_kernel_spmd(nc, in_maps, core_ids=list(range(NCORES)))
    out = np.empty((NCORES, S, D), np.float32)
    for b in range(NCORES):
        yT = np.asarray(res.results[b]["yT"]).reshape(D, S)
        out[b] = yT.T
    return out
```
